# Optimizing a Trainium2 kernel written in Bass

```python
import jax, jax.numpy as jnp
from jax import lax
import numpy as np

D_MODEL = 1024
BATCH = 2
SEQ = 8192
DEPTH = 2

HEAD_DIM = 64
D_MIX = D_MODEL
D_SB = D_MIX // 2
SB_HEADS = D_SB // HEAD_DIM
D_CONV = D_MIX // 4
CONV_GROUPS = D_CONV // HEAD_DIM
CONV_WIDTH = 3
D_POOL = D_MIX - D_SB - D_CONV
POOL_WINDOWS = (2, 4, 8, 16)
POOL_GROUPS = len(POOL_WINDOWS)
POOL_GROUP_DIM = D_POOL // POOL_GROUPS
D_IN = 3 * D_SB + 3 * D_CONV + D_POOL
SPLITS = (D_SB, 2 * D_SB, 3 * D_SB, 3 * D_SB + D_CONV, 3 * D_SB + 2 * D_CONV, 3 * D_SB + 3 * D_CONV)
D_FF = 4 * D_MODEL
Q_BLOCK = 128
DEEPNORM_ALPHA = (2 * DEPTH) ** 0.25
DEEPNORM_BETA = (8 * DEPTH) ** -0.25
LN_EPS = 1e-5
RMS_EPS = 1e-6

kernel_name = "hybrid_sb_attn_shortconv_pool_deepnorm"


def layer_norm(x, g, b):
    xf = x.astype(jnp.float32)
    mu = jnp.mean(xf, axis=-1, keepdims=True)
    xc = xf - mu
    var = jnp.mean(xc * xc, axis=-1, keepdims=True)
    y = xc * lax.rsqrt(var + LN_EPS) * g.astype(jnp.float32) + b.astype(jnp.float32)
    return y.astype(x.dtype)


def head_group_rmsnorm(o, gain):
    B, S, C = o.shape
    of = o.astype(jnp.float32).reshape(B, S, C // HEAD_DIM, HEAD_DIM)
    of = of * lax.rsqrt(jnp.mean(of * of, axis=-1, keepdims=True) + RMS_EPS)
    return (of.reshape(B, S, C) * gain.astype(jnp.float32)).astype(o.dtype)


def stick_breaking_attention(q, k, v):
    B, S, H, Dh = q.shape
    dtype = q.dtype
    scale = Dh ** -0.5
    qf = q.astype(jnp.float32).transpose(0, 2, 1, 3)
    kf = k.astype(jnp.float32).transpose(0, 2, 1, 3)
    vf = v.astype(jnp.float32).transpose(0, 2, 1, 3)
    outs = []
    for start in range(0, S, Q_BLOCK):
        end = start + Q_BLOCK
        qb = qf[:, :, start:end]
        kb = kf[:, :, :end]
        vb = vf[:, :, :end]
        z = jnp.einsum('bhtd,bhsd->bhts', qb, kb) * scale
        t_pos = jnp.arange(start, end)[:, None]
        s_pos = jnp.arange(end)[None, :]
        mask = s_pos < t_pos
        log_om = jnp.where(mask, jax.nn.log_sigmoid(-z), 0.0)
        tail = lax.cumsum(log_om, axis=3, reverse=True) - log_om
        log_a = jax.nn.log_sigmoid(z) + tail
        a = jnp.where(mask, jnp.exp(log_a), 0.0)
        outs.append(jnp.einsum('bhts,bhsd->bhtd', a, vb))
    o = jnp.concatenate(outs, axis=2)
    return o.transpose(0, 2, 1, 3).reshape(B, S, H * Dh).astype(dtype)


def short_conv_mixer(b_gate, c_gate, h, conv_w):
    u = c_gate * h
    S = u.shape[1]
    u_pad = jnp.pad(u, ((0, 0), (CONV_WIDTH - 1, 0), (0, 0)))
    y = u_pad[:, 0:S] * conv_w[0]
    for i in range(1, CONV_WIDTH):
        y = y + u_pad[:, i:i + S] * conv_w[i]
    return b_gate * y


def multiscale_pool_mixer(p, pool_w, pool_scale):
    B, S, C = p.shape
    dtype = p.dtype
    pg = p.astype(jnp.float32).reshape(B, S, POOL_GROUPS, POOL_GROUP_DIM)
    cs = jnp.pad(jnp.cumsum(pg, axis=1), ((0, 0), (1, 0), (0, 0), (0, 0)))
    pos = jnp.arange(S)
    outs = []
    for g, w in enumerate(POOL_WINDOWS):
        lo = jnp.maximum(pos + 1 - w, 0)
        window_sum = cs[:, 1:, g] - cs[:, lo, g]
        count = (pos + 1 - lo).astype(jnp.float32)[None, :, None]
        outs.append(window_sum / count - pg[:, :, g])
    pooled = jnp.stack(outs, axis=2)
    y = jnp.einsum('bsgc,gcd->bsgd', pooled, pool_w.astype(jnp.float32))
    y = y.reshape(B, S, C) * pool_scale.astype(jnp.float32)
    return y.astype(dtype)


def setup_inputs(seed: int = 0) -> dict:
    key = jax.random.key(seed)
    ks = jax.random.split(key, 16)
    f32 = jnp.float32
    x = jax.random.normal(ks[0], (BATCH, SEQ, D_MODEL), f32)
    w_in = jax.random.normal(ks[1], (DEPTH, D_MODEL, D_IN), f32) * D_MODEL ** -0.5
    conv_w = jax.random.normal(ks[2], (DEPTH, CONV_WIDTH, D_CONV), f32) * CONV_WIDTH ** -0.5
    pool_w = jax.random.normal(ks[3], (DEPTH, POOL_GROUPS, POOL_GROUP_DIM, POOL_GROUP_DIM), f32) * POOL_GROUP_DIM ** -0.5
    pool_scale = 1.0 + 0.1 * jax.random.normal(ks[4], (DEPTH, D_POOL), f32)
    mix_norm_g = 1.0 + 0.02 * jax.random.normal(ks[5], (DEPTH, D_MIX), f32)
    w_o = jax.random.normal(ks[6], (DEPTH, D_MIX, D_MODEL), f32) * (D_MIX ** -0.5) * DEEPNORM_BETA
    ln1_g = 1.0 + 0.02 * jax.random.normal(ks[7], (DEPTH, D_MODEL), f32)
    ln1_b = 0.02 * jax.random.normal(ks[8], (DEPTH, D_MODEL), f32)
    w_up = jax.random.normal(ks[9], (DEPTH, D_MODEL, D_FF), f32) * D_MODEL ** -0.5
    w_down = jax.random.normal(ks[10], (DEPTH, D_FF, D_MODEL), f32) * (D_FF ** -0.5) * DEEPNORM_BETA
    ln2_g = 1.0 + 0.02 * jax.random.normal(ks[11], (DEPTH, D_MODEL), f32)
    ln2_b = 0.02 * jax.random.normal(ks[12], (DEPTH, D_MODEL), f32)
    return {"x": x, "w_in": w_in, "conv_w": conv_w, "pool_w": pool_w,
            "pool_scale": pool_scale, "mix_norm_g": mix_norm_g, "w_o": w_o,
            "ln1_g": ln1_g, "ln1_b": ln1_b, "w_up": w_up, "w_down": w_down,
            "ln2_g": ln2_g, "ln2_b": ln2_b}


def reference(x, w_in, conv_w, pool_w, pool_scale, mix_norm_g, w_o,
              ln1_g, ln1_b, w_up, w_down, ln2_g, ln2_b):
    B, S, _ = x.shape
    for l in range(DEPTH):
        proj = jnp.einsum('bsd,de->bse', x, w_in[l])
        q, k, v, b_gate, c_gate, h, p = jnp.split(proj, SPLITS, axis=-1)
        q = q.reshape(B, S, SB_HEADS, HEAD_DIM)
        k = k.reshape(B, S, SB_HEADS, HEAD_DIM)
        v = v.reshape(B, S, SB_HEADS, HEAD_DIM)
        attn_out = stick_breaking_attention(q, k, v)
        conv_out = short_conv_mixer(b_gate, c_gate, h, conv_w[l])
        pool_out = multiscale_pool_mixer(p, pool_w[l], pool_scale[l])
        mix = jnp.concatenate([attn_out, conv_out, pool_out], axis=-1)
        mix = head_group_rmsnorm(mix, mix_norm_g[l])
        mix = jnp.einsum('bse,ed->bsd', mix, w_o[l])
        x = layer_norm(DEEPNORM_ALPHA * x + mix, ln1_g[l], ln1_b[l])
        hid = jnp.square(jax.nn.relu(jnp.einsum('bsd,df->bsf', x, w_up[l])))
        ff = jnp.einsum('bsf,fd->bsd', hid, w_down[l])
        x = layer_norm(DEEPNORM_ALPHA * x + ff, ln2_g[l], ln2_b[l])
    return x
```

```python
import numpy as np
import concourse.bass as bass
import concourse.mybir as mybir
from concourse.bass_utils import run_bass_kernel_spmd
from contextlib import ExitStack

F32 = mybir.dt.float32
BF16 = mybir.dt.bfloat16
AF = mybir.ActivationFunctionType
ALU = mybir.AluOpType

D = 1024
DFF = 4096
DIN = 2560
DEPTH = 2
DEBUG = False
FUSED = True
SKIPLO = True
CSENG = "dve"
ALPHA = float((2 * DEPTH) ** 0.25)
LN_EPS = 1e-5
RMS_EPS = 1e-6
NEG = -30000.0
ENG = ("pe", "act", "dve", "pool", "sp")


class Plan:
    def __init__(self, nc, es):
        self.nc, self.es = nc, es
        self.q = {e: [] for e in ENG}
        self.cnt = {e: 0 for e in ENG}
        self.nsem = 0
        self.sem = {e: self._newsem() for e in ENG}
        self.last_w = {}
        self.readers = {}
        self.waited = {e: {} for e in ENG}
        self.dma_sem = {}
        self.all_tokens = {}

    def _newsem(self):
        self.nsem += 1
        return self.es.enter_context(self.nc.semaphore("s%d" % self.nsem))

    def _prune(self, eng, waits):
        out = []
        w = self.waited[eng]
        for (s, v) in waits:
            if w.get(id(s), (None, 0))[1] >= v:
                continue
            w[id(s)] = (s, v)
            out.append((s, v))
        return out

    def op(self, eng, fn, reads=(), writes=(), dma=None, dma_amt=16):
        waits = []
        for k in reads:
            if k in self.last_w:
                waits.append(self.last_w[k])
        for k in writes:
            if k in self.last_w:
                waits.append(self.last_w[k])
            waits.extend(self.readers.get(k, ()))
        if eng == "pe":
            waits = [x for x in waits if x[0] is not self.sem["pe"]]
        waits = self._prune(eng, waits)
        if dma is not None:
            if dma not in self.dma_sem:
                self.dma_sem[dma] = [self._newsem(), 0]
            ds = self.dma_sem[dma]
            ds[1] += dma_amt
            tok = (ds[0], ds[1])
            amt = dma_amt
        else:
            if self.cnt[eng] >= 20000:
                self.sem[eng] = self._newsem()
                self.cnt[eng] = 0
            self.cnt[eng] += 1
            tok = (self.sem[eng], self.cnt[eng])
            amt = 1
        self.all_tokens[id(tok[0])] = tok
        for k in reads:
            self.readers.setdefault(k, []).append(tok)
        for k in writes:
            self.last_w[k] = tok
            self.readers[k] = []
        self.q[eng].append((fn, waits, tok[0], amt))
        return tok

    def barrier(self, skip_dma_keys=()):
        skip = set(id(self.dma_sem[k][0]) for k in skip_dma_keys if k in self.dma_sem)
        toks = [t for t in self.all_tokens.values() if id(t[0]) not in skip]
        for e in ENG:
            w = self._prune(e, toks)
            if w:
                self.q[e].append((None, w, None, 0))
        keep = {k: v for k, v in self.last_w.items() if isinstance(k, tuple) and k[0] == "gall"} if skip else {}
        self.last_w.clear()
        self.readers.clear()
        self.last_w.update(keep)

    def final_wait(self, eng="sp"):
        toks = list(self.all_tokens.values())
        w = self._prune(eng, toks)
        self.q[eng].append((None, w, None, 0))

    def emit(self, eng, e):
        for (fn, waits, sem, amt) in self.q[eng]:
            for (s, v) in waits:
                e.wait_ge(s, v)
            if fn is not None:
                ins = fn(e)
                ins.then_inc(sem, amt)


def build_nc(NG, NL):
    T = 256 * NG
    S = 1024 * NG
    NB = 2 * NG
    NKB = S // 128
    NH = NB * 16
    NT5 = max(1, T // 512)
    TW = min(512, T)
    XW = TW
    BPX = XW // 128

    def sigma(p):
        c8 = p % 8
        ccp = c8 if c8 < 4 else 7 - c8
        return ccp * NB + 2 * (p // 8) + (0 if c8 < 4 else 1)
    nc = bass.Bass("TRN2", target_bir_lowering=False)
    dt = nc.dram_tensor

    x_own = dt("x_own", [T, D], F32, kind="ExternalInput")
    xT_own = dt("xT_own", [D, T], F32, kind="ExternalInput")
    xT_halo = dt("xT_halo", [D, NH], F32, kind="ExternalInput")
    xT_full = dt("xT_full", [D, S], F32, kind="ExternalInput")
    invcnt = dt("invcnt", [128, 2 * NB * 128], F32, kind="ExternalInput")
    masks = dt("masks", [128, 8 * 256], F32, kind="ExternalInput")
    consts = dt("consts", [128, 4 * 128], F32, kind="ExternalInput")
    w_in = dt("w_in", [NL, D, DIN], F32, kind="ExternalInput")
    conv_w = dt("conv_w", [NL, 3, 256], F32, kind="ExternalInput")
    pool_w = dt("pool_w", [NL, 4, 64, 64], F32, kind="ExternalInput")
    pool_scale = dt("pool_scale", [NL, 256], F32, kind="ExternalInput")
    mix_g = dt("mix_norm_g", [NL, D], F32, kind="ExternalInput")
    w_o = dt("w_o", [NL, D, D], F32, kind="ExternalInput")
    ln1_g = dt("ln1_g", [NL, D], F32, kind="ExternalInput")
    ln1_b = dt("ln1_b", [NL, D], F32, kind="ExternalInput")
    w_up = dt("w_up", [NL, D, DFF], F32, kind="ExternalInput")
    w_down = dt("w_down", [NL, DFF, D], F32, kind="ExternalInput")
    ln2_g = dt("ln2_g", [NL, D], F32, kind="ExternalInput")
    ln2_b = dt("ln2_b", [NL, D], F32, kind="ExternalInput")
    bcw = dt("bcw", [128, 16], F32, kind="ExternalInput")
    out = dt("out", [T, D], F32, kind="ExternalOutput")
    LW = 8 * T + 8 * NH
    if NL > 1:
        xsp = dt("xsp", [T, D], F32)
        CWS = [2 * T] * 4 + [8 * NH]
        locs = [dt("loc%d" % i, [128, CWS[i]], BF16) for i in range(5)]
        galls = [dt("gall%d" % i, [4 * 128, CWS[i]], BF16) for i in range(5)]
    if DEBUG:
        dbg_mixn = dt("dbg_mixn", [128, 8 * T], BF16, kind="ExternalOutput")
        dbg_x1 = dt("dbg_x1", [T, D], F32, kind="ExternalOutput")
        dbg_qt = dt("dbg_qt", [128, 4 * T], BF16, kind="ExternalOutput")

    es = ExitStack()
    with es:
        off = [17408]

        def region(nbytes):
            o = off[0]
            off[0] += (nbytes + 63) // 64 * 64
            return o

        def at(name, shape, dtype, o):
            return nc.alloc_sbuf_tensor_at(name, shape, dtype, offset=o)

        R1 = region(max(8 * T * 2, 16384))
        R2 = region(max(2 * S * 2 + NKB * 256 * 2, NB * 1024 * 4))
        R3 = region(max(4 * T * 2, 2 * 8 * TW * 2, 8192))
        R4 = region(max(8 * T * 2, 32768))
        R5 = region(32768)
        R6 = region(31040)
        assert off[0] <= 229376, off[0]

        xTo = at("xTo", [128, 8, T], BF16, R1)
        x1T = xTo
        KT = at("KT", [128, 2, S], BF16, R2)
        Vt = at("Vt", [128, NKB, 256], BF16, R2 + 2 * S * 2)
        x1 = at("x1", [128, NB, 1024], F32, R2)
        CPW = NB * 144
        cB = at("cB", [128, NB, 144], F32, R2)
        cC = at("cC", [128, NB, 144], F32, R2 + CPW * 4)
        cH = at("cH", [128, NB, 144], F32, R2 + 2 * CPW * 4)
        cY = at("cY", [128, T], F32, R2 + 3 * CPW * 4)
        pbf = at("pbf", [128, T], BF16, R2 + 3 * CPW * 4 + T * 4)
        inv = at("inv", [128, 2, NB, 128], F32, R2 + 3 * CPW * 4 + T * 4 + T * 2)
        assert 3 * CPW * 4 + T * 4 + T * 2 + 2 * NB * 128 * 4 <= max(2 * S * 2 + NKB * 256 * 2, NB * 1024 * 4)
        QT = at("QT", [128, 4, T], BF16, R3)
        hid = at("hid", [128, 2, 8, TW], BF16, R3)
        mixn = at("mixn", [128, 8, T], BF16, R4)
        wu = at("wu", [128, 2, 8, 1024], BF16, R4)
        Eb = at("Eb", [128, 2, 1024], F32, R5)
        Lb = at("Lb", [128, 2, 1024], BF16, R5 + 8192)
        Ab = at("Ab", [128, 2, 1024], BF16, R5 + 12288)
        Cs = at("Cs", [128, 3, 1024], BF16, R5 + 16384)
        yat = at("yat", [128, 512], F32, R5 + 22528)
        sqb = at("sqb", [128, 512], BF16, R5 + 24576)
        lnv = at("lnv", [128, 512], F32, R5 + 25600)
        rsb = at("rsb", [128, 512], F32, R5 + 27648)
        r32 = at("r32", [128, 2, 512], F32, R6)
        wo = at("wo", [128, 8, 1024], BF16, R5)
        wd = at("wd", [128, 2, 8, 1024], BF16, R5)
        wsl = at("wsl", [128, 2, 8, 512], BF16, R6)
        xs = at("xs", [128, 2, 8, 512], BF16, R6)
        wsl2 = at("wsl2", [128, 8, 512], BF16, R5)
        msk = at("msk", [128, 8, 256], BF16, R6 + 16384)
        wkv = at("wkv", [128, 2, 8, 256], BF16, R6 + 20480)
        gb = at("gb", [128, 2, 1024], F32, R6 + 20480)
        xh = at("xh", [128, 8, NH], BF16, R6 + 16384)
        cst = at("cst", [128, 4, 128], BF16, R6 + 28672)
        idf = at("idf", [128, 128], F32, R6 + 29696)
        PW = at("PW", [128, 2, 128], BF16, R6 + 30208)
        vec = at("vec", [128, 48], F32, R6 + 30720)
        stt = at("stt", [128, 32], F32, R6 + 30912)
        xsB = at("xsB", [128, 2, 8, 512], BF16, R1)
        tls = at("tls", [128, 2, 8, NH], BF16, R6 + 20480)
        xin = at("xin2", [128, 2, 1024], F32, R3)
        assert NH * 16 <= 4096

        ps = es.enter_context(nc.psum_tensor("ps", [128, 4096], F32))

        def bank(i, n=1):
            return ps[:, i * 512:(i + n) * 512]

        def zw(i):
            return ps[:, (i % 2) * 1024:(i % 2) * 1024 + 1024]

        P = Plan(nc, es)

        def vcol(i):
            return vec[:, i:i + 1]

        vctr = [0]

        def NEXTV():
            vctr[0] += 1
            return vctr[0] % 16

        VECS = [("vecs", i) for i in range(16)]

        def dram_ap(h, offset, pat):
            return bass.AP(h, offset, pat)

        P.op("pool", lambda e: e.dma_start(out=cst[:], in_=consts.ap().rearrange("p (a b) -> p a b", b=128)),
             writes=["cst"], dma="cst")
        P.op("sp", lambda e: e.dma_start(out=idf[:], in_=consts.ap()[:, 0:128]), writes=["idf"], dma="idf")
        P.op("dve", lambda e: e.memset(vec[:, 16:17], RMS_EPS), writes=["veps1"])
        P.op("dve", lambda e: e.memset(vec[:, 17:18], LN_EPS), writes=["veps2"])
        P.op("sp", lambda e: e.dma_start(out=vec[:, 32:48], in_=bcw.ap()), writes=["bcw"], dma="bcw")
        IDENT = cst[:, 0, :]
        NU = cst[:, 1, :]
        NONES = cst[:, 2, :]
        BONES = cst[:, 3, :]

        gpc = [0]

        def gp_bank():
            gpc[0] += 1
            return 6 + (gpc[0] % 2)

        def rmsnorm(ykey, y_ap, n, outs, l):
            P.op("dve", lambda e: e.tensor_tensor(out=sqb[:, 0:n], in0=y_ap, in1=y_ap, op=ALU.mult),
                 reads=[ykey], writes=["sqb"])
            b = gp_bank()
            pb = bank(b)
            P.op("pe", lambda e: e.matmul(pb[:, 0:n], BONES, sqb[:, 0:n], start=True, stop=True),
                 reads=["sqb", "cst"], writes=[("bank", b)])
            P.op("act", lambda e: e.activation(out=lnv[:, 0:n], in_=pb[:, 0:n], func=AF.Ln, bias=vcol(16), scale=1.0 / 64),
                 reads=[("bank", b), "veps1"], writes=["lnv"])
            P.op("act", lambda e: e.activation(out=rsb[:, 0:n], in_=lnv[:, 0:n], func=AF.Exp, scale=-0.5),
                 reads=["lnv"], writes=["rsb"])
            for (c0, c1, o_ap, gi, okey) in outs:
                P.op("dve", lambda e, c0=c0, c1=c1, o_ap=o_ap, gi=gi: e.scalar_tensor_tensor(
                    out=o_ap, in0=y_ap[:, c0:c1], scalar=vcol(8 + gi), in1=rsb[:, c0:c1], op0=ALU.mult, op1=ALU.mult),
                    reads=[ykey, "rsb", "vgain"], writes=[okey])

        def ln_stats(b):
            key = ("x1", b)
            sb_ = b % 2
            st_ = stt[:, sb_ * 16:(sb_ + 1) * 16]
            for hf in range(2):
                P.op("dve", lambda e, hf=hf: e.bn_stats(out=st_[:, hf * 6:(hf + 1) * 6], in_=x1[:, b, hf * 512:(hf + 1) * 512]),
                     reads=[key], writes=[("stt", sb_, hf)])
            P.op("dve", lambda e: e.bn_aggr(out=st_[:, 12:14], in_=st_[:, 0:12]),
                 reads=[("stt", sb_, 0), ("stt", sb_, 1)], writes=[("mv", sb_)])
            P.op("act", lambda e: e.activation(out=st_[:, 14:15], in_=st_[:, 13:14], func=AF.Ln, bias=vcol(17), scale=1.0),
                 reads=[("mv", sb_), "veps2"], writes=[("lv", sb_)])
            P.op("act", lambda e: e.activation(out=st_[:, 15:16], in_=st_[:, 14:15], func=AF.Exp, scale=-0.5),
                 reads=[("lv", sb_)], writes=[("rstd", sb_)])

        def ln_apply(b, gkey):
            key = ("x1", b)
            sb_ = b % 2
            st_ = stt[:, sb_ * 16:(sb_ + 1) * 16]
            xb = x1[:, b, :]
            P.op("dve", lambda e: e.scalar_tensor_tensor(out=xb, in0=xb, scalar=st_[:, 12:13], in1=gb[:, 0, :],
                                                         op0=ALU.subtract, op1=ALU.mult),
                 reads=[key, ("mv", sb_), "gb", gkey], writes=[key])
            P.op("dve", lambda e: e.scalar_tensor_tensor(out=xb, in0=xb, scalar=st_[:, 15:16], in1=gb[:, 1, :],
                                                         op0=ALU.mult, op1=ALU.add),
                 reads=[key, ("rstd", sb_), gkey], writes=[key, ("stt", sb_, 0), ("stt", sb_, 1), ("mv", sb_)])

        def prefetch(lx):
            for i in range(3):
                for k in range(2):
                    P.op("sp", lambda e, i=i, k=k: e.dma_start(
                        out=vec[:, i * 2 + k:i * 2 + k + 1],
                        in_=conv_w.ap()[lx, i, k * 128:(k + 1) * 128].rearrange("(p o) -> p o", o=1)),
                        writes=[("vecs", NEXTV())], dma="vec")
            for k in range(2):
                P.op("sp", lambda e, k=k: e.dma_start(
                    out=vec[:, 6 + k:7 + k], in_=pool_scale.ap()[lx, k * 128:(k + 1) * 128].rearrange("(p o) -> p o", o=1)),
                    writes=[("vecs", NEXTV())], dma="vec")
            for k in range(8):
                P.op("sp", lambda e, k=k: e.dma_start(
                    out=vec[:, 20 + k:21 + k], in_=mix_g.ap()[lx, k * 128:(k + 1) * 128].rearrange("(p o) -> p o", o=1)),
                    writes=[("vecs", NEXTV())], dma="vec")
            wv_ = w_in.ap()[lx].rearrange("(k p) c -> p k c", p=128)
            P.op("pool", lambda e: e.dma_start(out=wsl[:, 0, :, :], in_=wv_[:, :, 0:512]),
                 writes=[("wsl", 0), ("r32", 0), ("r32", 1)], dma=("wsl", 0))
            P.op("pool", lambda e: e.dma_start(out=wsl[:, 1, :, :], in_=wv_[:, :, 1536:2048]),
                 writes=[("wsl", 1)], dma=("wsl", 1))
            P.op("pool", lambda e: e.dma_start(out=wsl2[:], in_=wv_[:, :, 2048:2560]),
                 writes=[("wsl", 2), ("wd", 0)], dma=("wsl", 2))
            P.op("dve", lambda e: e.memset(PW[:], 0.0), writes=["PW"])
            for k in range(2):
                for h2 in range(2):
                    P.op("pool", lambda e, k=k, h2=h2: e.dma_start(
                        out=PW[h2 * 64:(h2 + 1) * 64, k, h2 * 64:(h2 + 1) * 64], in_=pool_w.ap()[lx, 2 * k + h2]),
                        writes=["PW"], dma="PW")
            P.op("dve", lambda e: e.tensor_scalar(out=vec[:, 8:16], in0=vec[:, 20:28], scalar1=1.0, scalar2=None, op0=ALU.mult),
                 reads=[*VECS], writes=["vgain"])

        def do_layer(l):
            if l == 0:
                prefetch(0)
            if l == 0:
                P.op("pool", lambda e: e.dma_start(out=xTo[:], in_=xT_own.ap().rearrange("(k p) t -> p k t", p=128)),
                     writes=["xTo"], dma="xTo")
                P.op("pool", lambda e: e.dma_start(out=xh[:], in_=xT_halo.ap().rearrange("(k p) t -> p k t", p=128)),
                     writes=["xh"], dma="xh")
            else:
                gvt = galls[4].ap().rearrange("(r p) w -> p r w", p=128)
                P.op("dve", lambda e: e.memset(xh[:], 0.0), writes=["xh"])

                def hv(ap3, par):
                    return ap3.rearrange("p k (g two t) -> p (k g) two t", two=2, t=16)[:, :, par, :]

                def hvk(ap3, k, par, g0, g1):
                    return ap3[:, k, :].rearrange("p (g two t) -> p g two t", two=2, t=16)[:, g0:g1, par, :]

                for rho in range(4):
                    tb = rho % 2
                    ccr = rho
                    P.op("sp", lambda e, rho=rho, tb=tb: e.dma_start(
                        out=tls[:, tb, :, :], in_=gvt[:, rho, :].rearrange("p (k t) -> p k t", k=8)),
                        reads=[("gall", 4)], writes=[("tls", tb)], dma=("tls", tb))
                    tl = tls[:, tb, :, :]
                    cands = []
                    if ccr <= 2:
                        cands.append((ccr + 1, 0, 0))
                    if ccr >= 1:
                        cands.append((ccr - 1, 1, 1))
                    if ccr == 3:
                        cands.append((3, 1, 0))
                    for (wc, dp, sp_) in cands:
                        P.op("dve", lambda e, wc=wc, dp=dp, sp_=sp_, tl=tl: e.scalar_tensor_tensor(
                            out=hv(xh[:], dp), in0=hv(tl, sp_), scalar=vcol(40 + wc), in1=hv(xh[:], dp),
                            op0=ALU.mult, op1=ALU.add),
                            reads=[("tls", tb), "bcw", "xh"], writes=["xh"])
                    if ccr == 0 and NG > 1:
                        for k in range(8):
                            P.op("dve", lambda e, k=k, tl=tl: e.scalar_tensor_tensor(
                                out=hvk(xh[:], k, 0, 1, NG), in0=hvk(tl, k, 1, 0, NG - 1), scalar=vcol(40),
                                in1=hvk(xh[:], k, 0, 1, NG), op0=ALU.mult, op1=ALU.add),
                                reads=[("tls", tb), "bcw", "xh"], writes=["xh"])
            P.op("sp", lambda e: e.dma_start(out=inv[:], in_=invcnt.ap().rearrange("p (k b t) -> p k b t", k=2, t=128)),
                 writes=["inv"], dma="inv")
            def proj(si, cc, rhs_fn, n, evac, xkeys=("xTo",)):
                b = gp_bank()
                pb = bank(b)
                for k in range(8):
                    P.op("pe", lambda e, k=k: e.matmul(pb[:, 0:n], (wsl2[:, k, cc:cc + 128] if si == 2 else wsl[:, si, k, cc:cc + 128]), rhs_fn(k),
                                                      start=(k == 0), stop=(k == 7)),
                         reads=[("wsl", si), *xkeys], writes=[("bank", b)])
                evac(pb, ("bank", b))

            for j in range(4):
                for tt in range(NT5):
                    def ev(pb, key, j=j, tt=tt):
                        P.op("act", lambda e: e.activation(out=QT[:, j, tt * TW:(tt + 1) * TW], in_=pb[:, 0:TW],
                                                           func=AF.Identity, scale=0.125),
                             reads=[key], writes=[("QT", j, tt)])
                    proj(0, j * 128, lambda k, tt=tt: xTo[:, k, tt * TW:(tt + 1) * TW], TW, ev)
            BPT = TW // 128

            def proj_cp(si, cc, dst, dkey, own=True, halo=True):
                if own:
                    for tt in range(NT5):
                        def ev(pb, key, tt=tt):
                            P.op("act", lambda e: e.activation(
                                out=dst[:, tt * BPT:(tt + 1) * BPT, 16:144],
                                in_=pb[:, 0:TW].rearrange("p (b t) -> p b t", t=128), func=AF.Copy),
                                reads=[key], writes=[dkey])
                        proj(si, cc, lambda k, tt=tt: xTo[:, k, tt * TW:(tt + 1) * TW], TW, ev)
                if halo:
                    def ev2(pb, key):
                        P.op("dve", lambda e: e.tensor_copy(
                            out=dst[:, :, 0:16], in_=pb[:, 0:NH].rearrange("p (b t) -> p b t", t=16)),
                            reads=[key], writes=[dkey])
                    proj(si, cc, lambda k: xh[:, k, :], NH, ev2, xkeys=("xh",))

            for k in range(2):
                proj_cp(1, k * 128, cB, "cB", halo=False)
                proj_cp(1, 256 + k * 128, cC, "cC", halo=False)
                proj_cp(2, k * 128, cH, "cH", halo=False)
                proj_cp(1, 256 + k * 128, cC, "cC", own=False)
                proj_cp(2, k * 128, cH, "cH", own=False)
                P.op("dve", lambda e: e.tensor_tensor(out=cC[:], in0=cC[:], in1=cH[:], op=ALU.mult),
                     reads=["cC", "cH"], writes=["cC"])
                cYv = cY[:].rearrange("p (b t) -> p b t", t=128)
                P.op("dve", lambda e, k=k: e.tensor_scalar(out=cYv, in0=cC[:, :, 14:142], scalar1=vcol(0 + k), scalar2=None,
                                                           op0=ALU.mult),
                     reads=["cC", *VECS], writes=["cY"])
                P.op("dve", lambda e, k=k: e.scalar_tensor_tensor(out=cYv, in0=cC[:, :, 15:143], scalar=vcol(2 + k), in1=cYv,
                                                                  op0=ALU.mult, op1=ALU.add),
                     reads=["cC", "cY", *VECS], writes=["cY"])
                P.op("dve", lambda e, k=k: e.scalar_tensor_tensor(out=cYv, in0=cC[:, :, 16:144], scalar=vcol(4 + k), in1=cYv,
                                                                  op0=ALU.mult, op1=ALU.add),
                     reads=["cC", "cY", *VECS], writes=["cY"])
                P.op("dve", lambda e: e.tensor_tensor(out=cYv, in0=cYv, in1=cB[:, :, 16:144], op=ALU.mult),
                     reads=["cY", "cB"], writes=["cY"])
                for tt in range(NT5):
                    sl = slice(tt * TW, (tt + 1) * TW)
                    rmsnorm("cY", cY[:, sl], TW, [(0, TW, mixn[:, 4 + k, sl], 4 + k, ("mixn", 4 + k, tt))], l)
            for k in range(2):
                proj_cp(2, 256 + k * 128, cC, "cC")
                nlo = 2 * k + 1
                bufs = [cH, cB]
                bkeys = ["cH", "cB"]
                src, skey = cC, "cC"
                for j in range(nlo + 1):
                    dst_, dkey = bufs[j % 2], bkeys[j % 2]
                    sh = 1 << j
                    lo = (1 << (j + 1)) - 1
                    p0 = 0 if j < nlo else 64
                    P.op("dve", lambda e, dst_=dst_, src=src, sh=sh, lo=lo, p0=p0: e.tensor_tensor(
                        out=dst_[p0:128, :, lo:144], in0=src[p0:128, :, lo:144], in1=src[p0:128, :, lo - sh:144 - sh], op=ALU.add),
                        reads=[skey], writes=[dkey])
                    src, skey = dst_, dkey
                for (p0, p1, sb_, sk) in ((0, 64, cH, "cH"), (64, 128, cB, "cB")):
                    P.op("dve", lambda e, p0=p0, p1=p1, sb_=sb_, k=k: e.tensor_tensor(
                        out=sb_[p0:p1, :, 16:144], in0=sb_[p0:p1, :, 16:144], in1=inv[p0:p1, k, :, :], op=ALU.mult),
                        reads=[sk, "inv"], writes=[sk])
                    P.op("dve", lambda e, p0=p0, p1=p1, sb_=sb_: e.tensor_tensor(
                        out=pbf[p0:p1, :].rearrange("p (b t) -> p b t", t=128), in0=sb_[p0:p1, :, 16:144],
                        in1=cC[p0:p1, :, 16:144], op=ALU.subtract),
                        reads=[sk, "cC"], writes=["pbf"])
                for tt in range(NT5):
                    sl = slice(tt * TW, (tt + 1) * TW)
                    b = gp_bank()
                    pb = bank(b)
                    P.op("pe", lambda e, k=k, sl=sl, pb=pb: e.matmul(pb[:, 0:TW], PW[:, k, :], pbf[:, sl], start=True, stop=True),
                         reads=["pbf", "PW"], writes=[("bank", b)])
                    P.op("act", lambda e, k=k, sl=sl, pb=pb: e.activation(out=cY[:, sl], in_=pb[:, 0:TW], func=AF.Identity,
                                                                          scale=vcol(6 + k)),
                         reads=[("bank", b), *VECS], writes=["cY"])
                    rmsnorm("cY", cY[:, sl], TW, [(0, TW, mixn[:, 6 + k, sl], 6 + k, ("mixn", 6 + k, tt))], l)

            P.barrier()
            P.op("pool", lambda e: e.dma_start(out=msk[:], in_=masks.ap().rearrange("p (j t) -> p j t", t=256)),
                 writes=["msk"], dma="mskA")
            tile_ctr = [0]
            def c1_wkv(hx):
                P.op("pool", lambda e: e.dma_start(
                    out=wkv[:, 0, :, :], in_=w_in.ap()[l].rearrange("(k p) c -> p k c", p=128)[:, :, 512 + hx * 256:768 + hx * 256]),
                    writes=["wk"], dma="wk")
                P.op("pool", lambda e: e.dma_start(
                    out=wkv[:, 1, :, :], in_=w_in.ap()[l].rearrange("(k p) c -> p k c", p=128)[:, :, 1024 + hx * 256:1280 + hx * 256]),
                    writes=["wv"], dma="wv")

            def c1_xs(st, sb):
                if l == 0:
                    P.op("pool", lambda e: e.dma_start(
                        out=xs[:, sb, :, 0:XW], in_=xT_full.ap().rearrange("(k p) t -> p k t", p=128)[:, :, st * XW:(st + 1) * XW]),
                        writes=[("xs", sb)], dma=("xs", sb))
                else:
                    ccp, m = st // (T // XW), st % (T // XW)
                    for i4 in range(4):
                        gvi = galls[i4].ap().rearrange("(r p) w -> p r w", p=128)
                        P.op("sp", lambda e, i4=i4, gvi=gvi: e.dma_start(
                            out=xs[:, sb, 2 * i4:2 * i4 + 2, 0:XW],
                            in_=gvi[:, ccp, :].rearrange("p (k t) -> p k t", k=2)[:, :, m * XW:(m + 1) * XW]),
                            reads=[("gall", i4)], writes=[("xs", sb)], dma=("xsh", sb))

            def do_half(hh):
                pre = (hh == 1)
                if not pre:
                    c1_wkv(hh)
                TPR = T // XW
                order = [ccp * TPR + m for m in range(TPR) for ccp in range(4)]
                pending = []

                def k_piece(st, sb, jj):
                    b = gp_bank()
                    pb = bank(b)
                    for k in range(8):
                        P.op("pe", lambda e, k=k: e.matmul(
                            pb[:, 0:XW], wkv[:, 0, k, jj * 128:(jj + 1) * 128], xs[:, sb, k, 0:XW], start=(k == 0), stop=(k == 7)),
                            reads=["wk", ("xs", sb)], writes=[("bank", b)])
                    P.op("act", lambda e: e.activation(
                        out=KT[:, jj, st * XW:(st + 1) * XW], in_=pb[:, 0:XW], func=AF.Copy),
                        reads=[("bank", b)], writes=[("KT", st)])

                def v_piece(st, sb, b2):
                    b = gp_bank()
                    pb = bank(b)
                    for bl in range(2):
                        blk = b2 * 2 + bl
                        for k in range(8):
                            P.op("pe", lambda e, k=k, blk=blk, bl=bl: e.matmul(
                                pb[:, bl * 256:(bl + 1) * 256], xs[:, sb, k, blk * 128:(blk + 1) * 128], wkv[:, 1, k, :],
                                start=(k == 0 and bl == 0), stop=(k == 7), skip_group_check=True),
                                reads=["wv", ("xs", sb)], writes=[("bank", b)])
                    P.op("dve", lambda e: e.tensor_copy(
                        out=Vt[:, st * BPX + b2 * 2:st * BPX + b2 * 2 + 2, :], in_=pb[:, 0:512].rearrange("p (b c) -> p b c", c=256)),
                        reads=[("bank", b)], writes=[("V", st)])

                for idx, st in enumerate(order):
                    sb = idx % 2
                    lvl = idx // 4
                    if not (pre and idx < 2):
                        pending.append((lvl, lambda st=st, sb=sb: c1_xs(st, sb)))
                    for jj in range(2):
                        pending.append((lvl, lambda st=st, sb=sb, jj=jj: k_piece(st, sb, jj)))
                    for b2 in range(BPX // 2):
                        pending.append((lvl, lambda st=st, sb=sb, b2=b2: v_piece(st, sb, b2)))
                state = {"pref": False}

                def c1_pump(level_needed, trickle):
                    while pending and pending[0][0] <= level_needed:
                        pending.pop(0)[1]()
                    for _ in range(trickle):
                        if pending:
                            pending.pop(0)[1]()
                    if not pending and hh == 0 and not state["pref"]:
                        state["pref"] = True
                        c1_wkv(1)
                        c1_xs(order[0], 0)
                        c1_xs(order[1], 1)

                tiles = []
                for g in range(NG):
                    for kb in range(8 * g + 7, -1, -1):
                        tiles.append((g, kb))
                NTL = len(tiles)

                def hiv(ap2):
                    return ap2.rearrange("p (h c) -> p h c", c=256)[:, :, 128:256]

                def lov(ap2):
                    return ap2.rearrange("p (h c) -> p h c", c=256)[:, :, 0:128]

                def stage1a(i):
                    g, kb = tiles[i]
                    Z = zw(i)
                    zk = ("zw", i % 2)
                    j = kb - 8 * g
                    ho = SKIPLO and j >= 4
                    o0 = 128 if ho else 0
                    q = slice(g * 256 + o0, g * 256 + 256)
                    sg = sigma(kb)
                    ks = slice(sg * 128, sg * 128 + 128)
                    ktk = ("KT", sg // BPX)
                    qk = [("QT", 2 * hh, (g * 256) // TW), ("QT", 2 * hh + 1, (g * 256) // TW)]
                    spec = [(0, 0, 0, True), (512, 64, 0, True), (256, 0, 1, False), (768, 64, 1, False)]
                    for (c0, p0, ch, st_) in spec:
                        P.op("pe", lambda e, c0=c0, p0=p0, ch=ch, st_=st_: e.matmul(
                            Z[:, c0 + o0:c0 + 256], KT[p0:p0 + 64, ch, ks], QT[p0:p0 + 64, 2 * hh + ch, q],
                            start=st_, stop=False, skip_group_check=True),
                            reads=[ktk] + qk, writes=[zk])
                    if j >= 0:
                        for hb in range(4):
                            P.op("pe", lambda e, hb=hb, j=j: e.matmul(
                                Z[:, hb * 256 + o0:(hb + 1) * 256], IDENT, msk[:, j, o0:256], start=False, stop=False,
                                skip_group_check=True),
                                reads=["cst", "msk"], writes=[zk])
                    if ho:
                        P.op("act", lambda e: e.activation(out=hiv(Eb[:, i % 2, :]), in_=hiv(Z), func=AF.Exp),
                             reads=[zk], writes=[("E", i % 2)])
                    else:
                        P.op("act", lambda e: e.activation(out=Eb[:, i % 2, :], in_=Z, func=AF.Exp),
                             reads=[zk], writes=[("E", i % 2)])

                def stage1b(i):
                    g, kb = tiles[i]
                    first = (kb == 8 * g + 7)
                    j = kb - 8 * g
                    ho = SKIPLO and j >= 4
                    Li = Lb[:, i % 2, :]
                    Cn = Cs[:, i % 3, :]
                    Co = Cs[:, (i - 1) % 3, :]
                    if ho:
                        P.op("act", lambda e: e.activation(out=hiv(Li), in_=hiv(Eb[:, i % 2, :]), func=AF.Ln, bias=1.0),
                             reads=[("E", i % 2)], writes=[("L", i % 2)])
                    else:
                        P.op("act", lambda e: e.activation(out=Li, in_=Eb[:, i % 2, :], func=AF.Ln, bias=1.0),
                             reads=[("E", i % 2)], writes=[("L", i % 2)])
                    if first:
                        if ho:
                            P.op(CSENG, lambda e: e.tensor_copy(out=hiv(Cn), in_=hiv(Li)),
                                 reads=[("L", i % 2)], writes=[("Cs", i % 3)])
                        else:
                            P.op(CSENG, lambda e: e.tensor_copy(out=Cn, in_=Li),
                                 reads=[("L", i % 2)], writes=[("Cs", i % 3)])
                    elif ho:
                        P.op(CSENG, lambda e: e.tensor_tensor(out=hiv(Cn), in0=hiv(Co), in1=hiv(Li), op=ALU.add),
                             reads=[("L", i % 2), ("Cs", (i - 1) % 3)], writes=[("Cs", i % 3)])
                    elif SKIPLO and j == 3:
                        P.op(CSENG, lambda e: e.tensor_tensor(out=hiv(Cn), in0=hiv(Co), in1=hiv(Li), op=ALU.add),
                             reads=[("L", i % 2), ("Cs", (i - 1) % 3)], writes=[("Cs", i % 3)])
                        P.op(CSENG, lambda e: e.tensor_copy(out=lov(Cn), in_=lov(Li)),
                             reads=[("L", i % 2)], writes=[("Cs", i % 3)])
                    else:
                        P.op(CSENG, lambda e: e.tensor_tensor(out=Cn, in0=Co, in1=Li, op=ALU.add),
                             reads=[("L", i % 2), ("Cs", (i - 1) % 3)], writes=[("Cs", i % 3)])

                def stage2(i):
                    g, kb = tiles[i]
                    first = (kb == 8 * g + 7)
                    j = kb - 8 * g
                    ho = SKIPLO and j >= 4
                    Z = zw(i)
                    zk = ("zw", i % 2)
                    if ho:
                        for r4 in range(4):
                            cs_ = slice(r4 * 256 + 128, r4 * 256 + 256)
                            P.op("pe", lambda e, cs_=cs_: e.matmul(Z[:, cs_], NU, Lb[:, i % 2, cs_], start=False, stop=False,
                                                                  skip_group_check=True),
                                 reads=["cst", ("L", i % 2)], writes=[zk])
                    else:
                        for hb in range(2):
                            P.op("pe", lambda e, hb=hb: e.matmul(
                                Z[:, hb * 512:(hb + 1) * 512], NU, Lb[:, i % 2, hb * 512:(hb + 1) * 512],
                                start=False, stop=False, skip_group_check=True),
                                reads=["cst", ("L", i % 2)], writes=[zk])
                    if not first:
                        if ho or (SKIPLO and j == 3):
                            for r4 in range(4):
                                cs_ = slice(r4 * 256 + 128, r4 * 256 + 256)
                                P.op("pe", lambda e, cs_=cs_: e.matmul(Z[:, cs_], NONES, Cs[:, (i - 1) % 3, cs_], start=False, stop=False,
                                                                      skip_group_check=True),
                                     reads=["cst", ("Cs", (i - 1) % 3)], writes=[zk])
                        else:
                            for hb in range(2):
                                P.op("pe", lambda e, hb=hb: e.matmul(
                                    Z[:, hb * 512:(hb + 1) * 512], NONES, Cs[:, (i - 1) % 3, hb * 512:(hb + 1) * 512],
                                    start=False, stop=False, skip_group_check=True),
                                    reads=["cst", ("Cs", (i - 1) % 3)], writes=[zk])
                    if ho:
                        P.op("act", lambda e: e.activation(out=hiv(Ab[:, i % 2, :]), in_=hiv(Z), func=AF.Exp),
                             reads=[zk], writes=[("A", i % 2)])
                    else:
                        P.op("act", lambda e: e.activation(out=Ab[:, i % 2, :], in_=Z, func=AF.Exp),
                             reads=[zk], writes=[("A", i % 2)])

                def stage3(i):
                    g, kb = tiles[i]
                    first = (kb == 8 * g + 7)
                    last = (kb == 0)
                    j = kb - 8 * g
                    ho = SKIPLO and j >= 4
                    o0 = 128 if ho else 0
                    pob = 4 + (g % 2)
                    PO = bank(pob)
                    pk = ("bank", pob)
                    spec = [(0, 0, 0, 0, True), (64, 0, 64, 512, True), (0, 256, 128, 256, False), (64, 256, 192, 768, False)]
                    for (p0, c0, v0, a0, st_) in spec:
                        P.op("pe", lambda e, p0=p0, c0=c0, v0=v0, a0=a0, st_=st_: e.matmul(
                            PO[p0:p0 + 64, c0 + o0:c0 + 256], Vt[:, sigma(kb), v0:v0 + 64], Ab[:, i % 2, a0 + o0:a0 + 256],
                            start=(first and st_), stop=False, skip_group_check=True),
                            reads=[("V", sigma(kb) // BPX), ("A", i % 2)], writes=[pk])
                    if last:
                        P.op("dve", lambda e: e.tensor_copy(out=yat[:], in_=PO), reads=[pk], writes=["yat"])
                        tt = (g * 256) // TW
                        rmsnorm("yat", yat[:], 512,
                                [(0, 256, mixn[:, 2 * hh, g * 256:g * 256 + 256], 2 * hh, ("mixn", 2 * hh, tt)),
                                 (256, 512, mixn[:, 2 * hh + 1, g * 256:g * 256 + 256], 2 * hh + 1, ("mixn", 2 * hh + 1, tt))], l)

                for i in range(NTL + 2):
                    if i < NTL:
                        c1_pump(((2 * tiles[i][0] + 1) * 128) // XW, 1 if (pending and pending[0][0] <= 1) or i % 2 == 0 else 0)
                        stage1a(i)
                    if 0 <= i - 1 < NTL:
                        stage2(i - 1)
                    if i < NTL:
                        stage1b(i)
                    if 0 <= i - 2 < NTL:
                        stage3(i - 2)
                c1_pump(10 ** 6, 0)
                P.barrier()

            for hh_ in range(2):
                do_half(hh_)

            if DEBUG:
                P.op("sp", lambda e: e.dma_start(out=dbg_qt.ap().rearrange("p (k t) -> p k t", k=4), in_=QT[:]),
                     reads=[], dma="dbg3")
                P.barrier()
            P.op("pool", lambda e: e.dma_start(out=wo[:], in_=w_o.ap()[l].rearrange("(k p) c -> p k c", p=128)),
                 writes=["wo"], dma="wo")
            P.op("sp", lambda e: e.dma_start(out=gb[:, 0, :], in_=dram_ap(ln1_g, l * D, [[0, 128], [1, D]])), writes=["gb"], dma="gb0")
            P.op("sp", lambda e: e.dma_start(out=gb[:, 1, :], in_=dram_ap(ln1_b, l * D, [[0, 128], [1, D]])), writes=["gb1"], dma="gb1")

            def transposes(b):
                pt = zw(b)
                for k in range(8):
                    P.op("pe", lambda e, k=k: e.transpose(pt[:, k * 128:(k + 1) * 128], x1[:, b, k * 128:(k + 1) * 128], idf[:]),
                         reads=[("x1", b), "idf"], writes=[("zw", b % 2)])
                P.op("act", lambda e: e.activation(out=x1T[:, :, b * 128:(b + 1) * 128],
                                                   in_=pt.rearrange("p (k t) -> p k t", t=128), func=AF.Copy),
                     reads=[("zw", b % 2)], writes=[("x1T", b)])

            def d_s1(b):
                P.op("sp", lambda e, b=b: e.dma_start(out=xin[:, b % 2, :], in_=(x_own if l == 0 else xsp).ap()[b * 128:(b + 1) * 128, :]),
                     writes=[("xin", b % 2)], dma=("xin", b % 2))
                mo = ps[:, 2048 + (b % 2) * 1024:2048 + (b % 2) * 1024 + 1024]
                mk = ("mo", b % 2)
                for hf in range(2):
                    for k in range(8):
                        P.op("pe", lambda e, k=k, hf=hf, b=b, mo=mo: e.matmul(
                            mo[:, hf * 512:(hf + 1) * 512], mixn[:, k, b * 128:(b + 1) * 128], wo[:, k, hf * 512:(hf + 1) * 512],
                            start=(k == 0), stop=(k == 7)),
                            reads=["wo", ("mixn", k, (b * 128) // TW)], writes=[mk])

            def d_s2(b):
                mo = ps[:, 2048 + (b % 2) * 1024:2048 + (b % 2) * 1024 + 1024]
                mk = ("mo", b % 2)
                for hf in range(2):
                    P.op("dve", lambda e, b=b, hf=hf, mo=mo: e.scalar_tensor_tensor(
                        out=x1[:, b, hf * 512:(hf + 1) * 512], in0=xin[:, b % 2, hf * 512:(hf + 1) * 512], scalar=ALPHA,
                        in1=mo[:, hf * 512:(hf + 1) * 512], op0=ALU.mult, op1=ALU.add),
                        reads=[("xin", b % 2), mk], writes=[("x1", b)])
                ln_stats(b)

            def d_s3(b):
                ln_apply(b, "gb1")
                transposes(b)

            for i in range(NB + 2):
                if i < NB:
                    d_s1(i)
                if 0 <= i - 1 < NB:
                    d_s2(i - 1)
                if 0 <= i - 2 < NB:
                    d_s3(i - 2)
            if DEBUG:
                P.op("sp", lambda e: e.dma_start(out=dbg_mixn.ap().rearrange("p (k t) -> p k t", k=8), in_=mixn[:]),
                     reads=[], dma="dbg1")
                for b in range(NB):
                    P.op("sp", lambda e, b=b: e.dma_start(out=dbg_x1.ap()[b * 128:(b + 1) * 128, :], in_=x1[:, b, :]),
                         reads=[("x1", b)], dma="dbg2")
            P.barrier()

            for qq in range(4):
                wb = qq % 2
                P.op("pool", lambda e, qq=qq, wb=wb: e.dma_start(
                    out=wu[:, wb, :, :], in_=w_up.ap()[l].rearrange("(k p) f -> p k f", p=128)[:, :, qq * 1024:(qq + 1) * 1024]),
                    writes=[("wu", wb)], dma=("wu", wb))
                P.op("pool", lambda e, qq=qq, wb=wb: e.dma_start(
                    out=wd[:, wb, :, :], in_=w_down.ap()[l, qq * 1024:(qq + 1) * 1024, :].rearrange("(f p) d -> p f d", p=128)),
                    writes=[("wd", wb)], dma=("wd", wb))
                for tt in range(NT5):
                    hb_ = tt % 2
                    for fc in range(8):
                        b = gp_bank()
                        pb = bank(b)
                        for k in range(8):
                            P.op("pe", lambda e, k=k, fc=fc, wb=wb, tt=tt, pb=pb: e.matmul(
                                pb[:, 0:TW], wu[:, wb, k, fc * 128:(fc + 1) * 128], x1T[:, k, tt * TW:(tt + 1) * TW],
                                start=(k == 0), stop=(k == 7)),
                                reads=[("wu", wb)] + [("x1T", bb) for bb in range(tt * BPT, (tt + 1) * BPT)], writes=[("bank", b)])
                        rb = fc % 2
                        P.op("act", lambda e, rb=rb, pb=pb: e.activation(out=r32[:, rb, 0:TW], in_=pb[:, 0:TW], func=AF.Relu),
                             reads=[("bank", b)], writes=[("r32", rb)])
                        P.op("dve", lambda e, rb=rb, fc=fc, hb_=hb_: e.tensor_tensor(
                            out=hid[:, hb_, fc, :], in0=r32[:, rb, 0:TW], in1=r32[:, rb, 0:TW], op=ALU.mult),
                            reads=[("r32", rb)], writes=[("hid", hb_)])
                    for bl in range(BPT):
                        blk = tt * BPT + bl
                        for hf in range(2):
                            b = 4 + ((bl * 2 + hf) % 2)
                            pb = bank(b)
                            for fc in range(8):
                                P.op("pe", lambda e, fc=fc, bl=bl, hf=hf, hb_=hb_, wb=wb, pb=pb: e.matmul(
                                    pb[:, 0:512], hid[:, hb_, fc, bl * 128:(bl + 1) * 128], wd[:, wb, fc, hf * 512:(hf + 1) * 512],
                                    start=(fc == 0), stop=(fc == 7)),
                                    reads=[("hid", hb_), ("wd", wb)], writes=[("bank", b)])
                            xsl = x1[:, blk, hf * 512:(hf + 1) * 512]
                            if qq == 0:
                                P.op("dve", lambda e, xsl=xsl, pb=pb: e.scalar_tensor_tensor(
                                    out=xsl, in0=xsl, scalar=ALPHA, in1=pb[:, 0:512], op0=ALU.mult, op1=ALU.add),
                                    reads=[("bank", b), ("x1", blk)], writes=[("x1", blk)])
                            else:
                                P.op("dve", lambda e, xsl=xsl, pb=pb: e.tensor_tensor(out=xsl, in0=xsl, in1=pb[:, 0:512], op=ALU.add),
                                     reads=[("bank", b), ("x1", blk)], writes=[("x1", blk)])
            P.op("sp", lambda e: e.dma_start(out=gb[:, 0, :], in_=dram_ap(ln2_g, l * D, [[0, 128], [1, D]])), writes=["gb"], dma="gb0")
            P.op("sp", lambda e: e.dma_start(out=gb[:, 1, :], in_=dram_ap(ln2_b, l * D, [[0, 128], [1, D]])), writes=["gb2"], dma="gb1")
            if l < NL - 1:
                prefetch(l + 1)

            def fin(b):
                ln_apply(b, "gb2")
                if l == NL - 1:
                    P.op("sp", lambda e, b=b: e.dma_start(out=out.ap()[b * 128:(b + 1) * 128, :], in_=x1[:, b, :]),
                         reads=[("x1", b)], dma=("out", b % 4))
                else:
                    P.op("sp", lambda e, b=b: e.dma_start(out=xsp.ap()[b * 128:(b + 1) * 128, :], in_=x1[:, b, :]),
                         reads=[("x1", b)], dma=("xsp", b % 4))
                    transposes(b)

            for b in range(NB):
                ln_stats(b)
                if b >= 1:
                    fin(b - 1)
            fin(NB - 1)
            if l < NL - 1:
                allx = [("x1T", b) for b in range(NB)]
                for k in range(8):
                    P.op("sp", lambda e, k=k: e.dma_start(
                        out=locs[4].ap()[:, k * NH:(k + 1) * NH].rearrange("p (s t) -> p s t", t=16),
                        in_=xTo[:, k, :].rearrange("p (s t) -> p s t", t=128)[:, :, 112:128]),
                        reads=allx, writes=[("loc", 4)], dma=("loc", 4))
                for i4 in range(4):
                    P.op("sp", lambda e, i4=i4: e.dma_start(out=locs[i4].ap().rearrange("p (k t) -> p k t", k=2),
                                                            in_=xTo[:, 2 * i4:2 * i4 + 2, :]),
                         reads=allx, writes=[("loc", i4)], dma=("loc", i4))
                for i5 in (4, 0, 1, 2, 3):
                    P.op("pool", lambda e, i5=i5: e.collective_compute(
                        "AllGather", ALU.bypass, replica_groups=[[0, 1, 2, 3], [4, 5, 6, 7]],
                        ins=[locs[i5].ap()], outs=[galls[i5].ap()]),
                        reads=[("loc", i5)], writes=[("gall", i5)], dma=("cc", i5), dma_amt=1)
                P.barrier(skip_dma_keys=[("cc", i5) for i5 in range(5)])
            else:
                P.barrier()

        for l_ in range(NL):
            do_layer(l_)
        P.final_wait("sp")

        block = es.enter_context(nc.Block())
        block.tensor(lambda e: P.emit("pe", e))
        block.scalar(lambda e: P.emit("act", e))
        block.vector(lambda e: P.emit("dve", e))
        block.gpsimd(lambda e: P.emit("pool", e))
        block.sync(lambda e: P.emit("sp", e))
    return nc


def _consts():
    c = np.zeros((128, 4, 128), np.float32)
    c[:, 0, :] = np.eye(128, dtype=np.float32)
    jj = np.arange(128)[:, None]
    ss = np.arange(128)[None, :]
    c[:, 1, :] = -(jj >= ss).astype(np.float32)
    c[:, 2, :] = -1.0
    bo = np.zeros((128, 128), np.float32)
    bo[:64, :64] = 1.0
    bo[64:, 64:] = 1.0
    c[:, 3, :] = bo
    return c.reshape(128, 512)


def _core_layout(NG, cc):
    pos = []
    for g in range(NG):
        pos += [8 * g + cc, 8 * g + 7 - cc]
    return pos


def _masks(cc):
    m = np.zeros((128, 8, 256), np.float32)
    s = np.arange(128)[:, None]
    t = np.arange(128)[None, :]
    tri = np.where(s < t, 0.0, NEG).astype(np.float32)
    for j in range(8):
        for half, qpos in ((0, cc), (1, 7 - cc)):
            sl = slice(half * 128, (half + 1) * 128)
            if j < qpos:
                m[:, j, sl] = 0.0
            elif j == qpos:
                m[:, j, sl] = tri
            else:
                m[:, j, sl] = NEG
    return m.reshape(128, 8 * 256)


def _bcw(b, cc):
    w = np.zeros((128, 16), np.float32)
    w[:, b] = 1.0
    w[:, 8 + cc] = 1.0
    return w


def _invcnt(NG, cc):
    pos = _core_layout(NG, cc)
    NB = len(pos)
    inv = np.zeros((128, 2, NB, 128), np.float32)
    for k in range(2):
        for h2 in range(2):
            w = 1 << (2 * k + h2 + 1)
            for bi, p in enumerate(pos):
                tpos = p * 128 + np.arange(128)
                cnt = np.minimum(tpos + 1, w).astype(np.float32)
                inv[h2 * 64:(h2 + 1) * 64, k, bi, :] = (1.0 / cnt)[None, :]
    return inv.reshape(128, 2 * NB * 128)


def _run_layers(x, wts, NG, layers, nc_cache={}):
    B, S, _ = x.shape
    NL = len(layers)
    key = (NG, NL)
    if key not in nc_cache:
        nc_cache[key] = build_nc(NG, NL)
    nc = nc_cache[key]
    consts = _consts()
    in_maps = []
    lay = []
    for c in range(8):
        b, cc = c // 4, c % 4
        pos = _core_layout(NG, cc)
        lay.append((b, pos))
        xb = x[b]
        x_own = np.concatenate([xb[p * 128:(p + 1) * 128] for p in pos], axis=0)
        halo = []
        for p in pos:
            if p == 0:
                halo.append(np.zeros((16, D), np.float32))
            else:
                halo.append(xb[p * 128 - 16:p * 128])
        halo = np.concatenate(halo, axis=0)
        m = {
            "x_own": np.ascontiguousarray(x_own),
            "xT_own": np.ascontiguousarray(x_own.T),
            "xT_halo": np.ascontiguousarray(halo.T),
            "xT_full": np.ascontiguousarray(np.concatenate(
                [xb[p * 128:(p + 1) * 128] for c2 in range(4) for p in _core_layout(NG, c2)], axis=0).T),
            "bcw": _bcw(b, cc),
            "invcnt": _invcnt(NG, cc),
            "masks": _masks(cc),
            "consts": consts,
        }
        for k, v in wts.items():
            m[k] = np.ascontiguousarray(v[layers])
        in_maps.append(m)
    res = run_bass_kernel_spmd(nc, in_maps, core_ids=list(range(8)))
    if DEBUG:
        _run_layers.dbg = res.results
    outp = np.empty_like(x)
    for c in range(8):
        b, pos = lay[c]
        o = res.results[c]["out"]
        for i, p in enumerate(pos):
            outp[b, p * 128:(p + 1) * 128] = o[i * 128:(i + 1) * 128]
    return outp


def kernel(x, w_in, conv_w, pool_w, pool_scale, mix_norm_g, w_o, ln1_g, ln1_b, w_up, w_down, ln2_g, ln2_b):
    x = np.asarray(x, np.float32)
    wts = {"w_in": w_in, "conv_w": conv_w, "pool_w": pool_w, "pool_scale": pool_scale, "mix_norm_g": mix_norm_g,
           "w_o": w_o, "ln1_g": ln1_g, "ln1_b": ln1_b, "w_up": w_up, "w_down": w_down, "ln2_g": ln2_g, "ln2_b": ln2_b}
    wts = {k: np.asarray(v, np.float32) for k, v in wts.items()}
    S = x.shape[1]
    NG = S // 1024
    depth = wts["w_in"].shape[0]
    if FUSED:
        return _run_layers(x, wts, NG, list(range(depth)))
    for l in range(depth):
        x = _run_layers(x, wts, NG, [l])
    return x
```

```python
import numpy as np
import concourse.bass as bass
import concourse.mybir as mybir
from concourse.bass_utils import run_bass_kernel_spmd
from contextlib import ExitStack

F32 = mybir.dt.float32
BF16 = mybir.dt.bfloat16
AF = mybir.ActivationFunctionType
ALU = mybir.AluOpType

D = 1024
DFF = 4096
DIN = 2560
DEPTH = 2
DEBUG = False
FUSED = True
SKIPLO = True
CSENG = "dve"
ALPHA = float((2 * DEPTH) ** 0.25)
LN_EPS = 1e-5
RMS_EPS = 1e-6
NEG = -30000.0
ENG = ("pe", "act", "dve", "pool", "sp")


class Plan:
    def __init__(self, nc, es):
        self.nc, self.es = nc, es
        self.q = {e: [] for e in ENG}
        self.cnt = {e: 0 for e in ENG}
        self.nsem = 0
        self.sem = {e: self._newsem() for e in ENG}
        self.last_w = {}
        self.readers = {}
        self.waited = {e: {} for e in ENG}
        self.dma_sem = {}
        self.all_tokens = {}

    def _newsem(self):
        self.nsem += 1
        return self.es.enter_context(self.nc.semaphore("s%d" % self.nsem))

    def _prune(self, eng, waits):
        out = []
        w = self.waited[eng]
        for (s, v) in waits:
            if w.get(id(s), (None, 0))[1] >= v:
                continue
            w[id(s)] = (s, v)
            out.append((s, v))
        return out

    def op(self, eng, fn, reads=(), writes=(), dma=None, dma_amt=16):
        waits = []
        for k in reads:
            if k in self.last_w:
                waits.append(self.last_w[k])
        for k in writes:
            if k in self.last_w:
                waits.append(self.last_w[k])
            waits.extend(self.readers.get(k, ()))
        if eng == "pe":
            waits = [x for x in waits if x[0] is not self.sem["pe"]]
        waits = self._prune(eng, waits)
        if dma is not None:
            if dma not in self.dma_sem:
                self.dma_sem[dma] = [self._newsem(), 0]
            ds = self.dma_sem[dma]
            ds[1] += dma_amt
            tok = (ds[0], ds[1])
            amt = dma_amt
        else:
            if self.cnt[eng] >= 20000:
                self.sem[eng] = self._newsem()
                self.cnt[eng] = 0
            self.cnt[eng] += 1
            tok = (self.sem[eng], self.cnt[eng])
            amt = 1
        self.all_tokens[id(tok[0])] = tok
        for k in reads:
            self.readers.setdefault(k, []).append(tok)
        for k in writes:
            self.last_w[k] = tok
            self.readers[k] = []
        self.q[eng].append((fn, waits, tok[0], amt))
        return tok

    def barrier(self, skip_dma_keys=()):
        skip = set(id(self.dma_sem[k][0]) for k in skip_dma_keys if k in self.dma_sem)
        toks = [t for t in self.all_tokens.values() if id(t[0]) not in skip]
        for e in ENG:
            w = self._prune(e, toks)
            if w:
                self.q[e].append((None, w, None, 0))
        keep = {k: v for k, v in self.last_w.items() if isinstance(k, tuple) and k[0] == "gall"} if skip else {}
        self.last_w.clear()
        self.readers.clear()
        self.last_w.update(keep)

    def final_wait(self, eng="sp"):
        toks = list(self.all_tokens.values())
        w = self._prune(eng, toks)
        self.q[eng].append((None, w, None, 0))

    def emit(self, eng, e):
        for (fn, waits, sem, amt) in self.q[eng]:
            for (s, v) in waits:
                e.wait_ge(s, v)
            if fn is not None:
                ins = fn(e)
                ins.then_inc(sem, amt)


def build_nc(NG, NL):
    T = 256 * NG
    S = 1024 * NG
    NB = 2 * NG
    NKB = S // 128
    NH = NB * 16
    NT5 = max(1, T // 512)
    TW = min(512, T)
    XW = TW
    BPX = XW // 128

    def sigma(p):
        c8 = p % 8
        ccp = c8 if c8 < 4 else 7 - c8
        return ccp * NB + 2 * (p // 8) + (0 if c8 < 4 else 1)
    nc = bass.Bass("TRN2", target_bir_lowering=False)
    dt = nc.dram_tensor

    x_own = dt("x_own", [T, D], F32, kind="ExternalInput")
    xT_own = dt("xT_own", [D, T], F32, kind="ExternalInput")
    xT_halo = dt("xT_halo", [D, NH], F32, kind="ExternalInput")
    xT_full = dt("xT_full", [D, S], F32, kind="ExternalInput")
    invcnt = dt("invcnt", [128, 2 * NB * 128], F32, kind="ExternalInput")
    masks = dt("masks", [128, 8 * 256], F32, kind="ExternalInput")
    consts = dt("consts", [128, 4 * 128], F32, kind="ExternalInput")
    w_in = dt("w_in", [NL, D, DIN], F32, kind="ExternalInput")
    conv_w = dt("conv_w", [NL, 3, 256], F32, kind="ExternalInput")
    pool_w = dt("pool_w", [NL, 4, 64, 64], F32, kind="ExternalInput")
    pool_scale = dt("pool_scale", [NL, 256], F32, kind="ExternalInput")
    mix_g = dt("mix_norm_g", [NL, D], F32, kind="ExternalInput")
    w_o = dt("w_o", [NL, D, D], F32, kind="ExternalInput")
    ln1_g = dt("ln1_g", [NL, D], F32, kind="ExternalInput")
    ln1_b = dt("ln1_b", [NL, D], F32, kind="ExternalInput")
    w_up = dt("w_up", [NL, D, DFF], F32, kind="ExternalInput")
    w_down = dt("w_down", [NL, DFF, D], F32, kind="ExternalInput")
    ln2_g = dt("ln2_g", [NL, D], F32, kind="ExternalInput")
    ln2_b = dt("ln2_b", [NL, D], F32, kind="ExternalInput")
    bcw = dt("bcw", [128, 16], F32, kind="ExternalInput")
    out = dt("out", [T, D], F32, kind="ExternalOutput")
    LW = 8 * T + 8 * NH
    if NL > 1:
        xsp = dt("xsp", [T, D], F32)
        CWS = [2 * T] * 4 + [8 * NH]
        locs = [dt("loc%d" % i, [128, CWS[i]], BF16) for i in range(5)]
        galls = [dt("gall%d" % i, [4 * 128, CWS[i]], BF16) for i in range(5)]
    if DEBUG:
        dbg_mixn = dt("dbg_mixn", [128, 8 * T], BF16, kind="ExternalOutput")
        dbg_x1 = dt("dbg_x1", [T, D], F32, kind="ExternalOutput")
        dbg_qt = dt("dbg_qt", [128, 4 * T], BF16, kind="ExternalOutput")

    es = ExitStack()
    with es:
        off = [17408]

        def region(nbytes):
            o = off[0]
            off[0] += (nbytes + 63) // 64 * 64
            return o

        def at(name, shape, dtype, o):
            return nc.alloc_sbuf_tensor_at(name, shape, dtype, offset=o)

        R1 = region(max(8 * T * 2, 16384))
        R2 = region(max(2 * S * 2 + NKB * 256 * 2, NB * 1024 * 4))
        R3 = region(max(4 * T * 2, 2 * 8 * TW * 2, 8192))
        R4 = region(max(8 * T * 2, 32768))
        R5 = region(32768)
        R6 = region(31040)
        assert off[0] <= 229376, off[0]

        xTo = at("xTo", [128, 8, T], BF16, R1)
        x1T = xTo
        KT = at("KT", [128, 2, S], BF16, R2)
        Vt = at("Vt", [128, NKB, 256], BF16, R2 + 2 * S * 2)
        x1 = at("x1", [128, NB, 1024], F32, R2)
        CPW = NB * 144
        cB = at("cB", [128, NB, 144], F32, R2)
        cC = at("cC", [128, NB, 144], F32, R2 + CPW * 4)
        cH = at("cH", [128, NB, 144], F32, R2 + 2 * CPW * 4)
        cY = at("cY", [128, T], F32, R2 + 3 * CPW * 4)
        pbf = at("pbf", [128, T], BF16, R2 + 3 * CPW * 4 + T * 4)
        inv = at("inv", [128, 2, NB, 128], F32, R2 + 3 * CPW * 4 + T * 4 + T * 2)
        assert 3 * CPW * 4 + T * 4 + T * 2 + 2 * NB * 128 * 4 <= max(2 * S * 2 + NKB * 256 * 2, NB * 1024 * 4)
        QT = at("QT", [128, 4, T], BF16, R3)
        hid = at("hid", [128, 2, 8, TW], BF16, R3)
        mixn = at("mixn", [128, 8, T], BF16, R4)
        wu = at("wu", [128, 2, 8, 1024], BF16, R4)
        Eb = at("Eb", [128, 2, 1024], F32, R5)
        Lb = at("Lb", [128, 2, 1024], BF16, R5 + 8192)
        Ab = at("Ab", [128, 2, 1024], BF16, R5 + 12288)
        Cs = at("Cs", [128, 3, 1024], BF16, R5 + 16384)
        yat = at("yat", [128, 512], F32, R5 + 22528)
        sqb = at("sqb", [128, 512], BF16, R5 + 24576)
        lnv = at("lnv", [128, 512], F32, R5 + 25600)
        rsb = at("rsb", [128, 512], F32, R5 + 27648)
        r32 = at("r32", [128, 2, 512], F32, R6)
        wo = at("wo", [128, 8, 1024], BF16, R5)
        wd = at("wd", [128, 2, 8, 1024], BF16, R5)
        wsl = at("wsl", [128, 2, 8, 512], BF16, R6)
        xs = at("xs", [128, 2, 8, 512], BF16, R6)
        wsl2 = at("wsl2", [128, 8, 512], BF16, R5)
        msk = at("msk", [128, 8, 256], BF16, R6 + 16384)
        wkv = at("wkv", [128, 2, 8, 256], BF16, R6 + 20480)
        gb = at("gb", [128, 2, 1024], F32, R6 + 20480)
        xh = at("xh", [128, 8, NH], BF16, R6 + 16384)
        cst = at("cst", [128, 4, 128], BF16, R6 + 28672)
        idf = at("idf", [128, 128], F32, R6 + 29696)
        PW = at("PW", [128, 2, 128], BF16, R6 + 30208)
        vec = at("vec", [128, 48], F32, R6 + 30720)
        stt = at("stt", [128, 32], F32, R6 + 30912)
        xsB = at("xsB", [128, 2, 8, 512], BF16, R1)
        tls = at("tls", [128, 2, 8, NH], BF16, R6 + 20480)
        xin = at("xin2", [128, 2, 1024], F32, R3)
        assert NH * 16 <= 4096

        ps = es.enter_context(nc.psum_tensor("ps", [128, 4096], F32))

        def bank(i, n=1):
            return ps[:, i * 512:(i + n) * 512]

        def zw(i):
            return ps[:, (i % 2) * 1024:(i % 2) * 1024 + 1024]

        P = Plan(nc, es)

        def vcol(i):
            return vec[:, i:i + 1]

        vctr = [0]

        def NEXTV():
            vctr[0] += 1
            return vctr[0] % 16

        VECS = [("vecs", i) for i in range(16)]

        def dram_ap(h, offset, pat):
            return bass.AP(h, offset, pat)

        P.op("pool", lambda e: e.dma_start(out=cst[:], in_=consts.ap().rearrange("p (a b) -> p a b", b=128)),
             writes=["cst"], dma="cst")
        P.op("sp", lambda e: e.dma_start(out=idf[:], in_=consts.ap()[:, 0:128]), writes=["idf"], dma="idf")
        P.op("dve", lambda e: e.memset(vec[:, 16:17], RMS_EPS), writes=["veps1"])
        P.op("dve", lambda e: e.memset(vec[:, 17:18], LN_EPS), writes=["veps2"])
        P.op("sp", lambda e: e.dma_start(out=vec[:, 32:48], in_=bcw.ap()), writes=["bcw"], dma="bcw")
        IDENT = cst[:, 0, :]
        NU = cst[:, 1, :]
        NONES = cst[:, 2, :]
        BONES = cst[:, 3, :]

        gpc = [0]

        def gp_bank():
            gpc[0] += 1
            return 6 + (gpc[0] % 2)

        def rmsnorm(ykey, y_ap, n, outs, l):
            P.op("dve", lambda e: e.tensor_tensor(out=sqb[:, 0:n], in0=y_ap, in1=y_ap, op=ALU.mult),
                 reads=[ykey], writes=["sqb"])
            b = gp_bank()
            pb = bank(b)
            P.op("pe", lambda e: e.matmul(pb[:, 0:n], BONES, sqb[:, 0:n], start=True, stop=True),
                 reads=["sqb", "cst"], writes=[("bank", b)])
            P.op("act", lambda e: e.activation(out=lnv[:, 0:n], in_=pb[:, 0:n], func=AF.Ln, bias=vcol(16), scale=1.0 / 64),
                 reads=[("bank", b), "veps1"], writes=["lnv"])
            P.op("act", lambda e: e.activation(out=rsb[:, 0:n], in_=lnv[:, 0:n], func=AF.Exp, scale=-0.5),
                 reads=["lnv"], writes=["rsb"])
            for (c0, c1, o_ap, gi, okey) in outs:
                P.op("dve", lambda e, c0=c0, c1=c1, o_ap=o_ap, gi=gi: e.scalar_tensor_tensor(
                    out=o_ap, in0=y_ap[:, c0:c1], scalar=vcol(8 + gi), in1=rsb[:, c0:c1], op0=ALU.mult, op1=ALU.mult),
                    reads=[ykey, "rsb", "vgain"], writes=[okey])

        def ln_stats(b):
            key = ("x1", b)
            sb_ = b % 2
            st_ = stt[:, sb_ * 16:(sb_ + 1) * 16]
            for hf in range(2):
                P.op("dve", lambda e, hf=hf: e.bn_stats(out=st_[:, hf * 6:(hf + 1) * 6], in_=x1[:, b, hf * 512:(hf + 1) * 512]),
                     reads=[key], writes=[("stt", sb_, hf)])
            P.op("dve", lambda e: e.bn_aggr(out=st_[:, 12:14], in_=st_[:, 0:12]),
                 reads=[("stt", sb_, 0), ("stt", sb_, 1)], writes=[("mv", sb_)])
            P.op("act", lambda e: e.activation(out=st_[:, 14:15], in_=st_[:, 13:14], func=AF.Ln, bias=vcol(17), scale=1.0),
                 reads=[("mv", sb_), "veps2"], writes=[("lv", sb_)])
            P.op("act", lambda e: e.activation(out=st_[:, 15:16], in_=st_[:, 14:15], func=AF.Exp, scale=-0.5),
                 reads=[("lv", sb_)], writes=[("rstd", sb_)])

        def ln_apply(b, gkey):
            key = ("x1", b)
            sb_ = b % 2
            st_ = stt[:, sb_ * 16:(sb_ + 1) * 16]
            xb = x1[:, b, :]
            P.op("dve", lambda e: e.scalar_tensor_tensor(out=xb, in0=xb, scalar=st_[:, 12:13], in1=gb[:, 0, :],
                                                         op0=ALU.subtract, op1=ALU.mult),
                 reads=[key, ("mv", sb_), "gb", gkey], writes=[key])
            P.op("dve", lambda e: e.scalar_tensor_tensor(out=xb, in0=xb, scalar=st_[:, 15:16], in1=gb[:, 1, :],
                                                         op0=ALU.mult, op1=ALU.add),
                 reads=[key, ("rstd", sb_), gkey], writes=[key, ("stt", sb_, 0), ("stt", sb_, 1), ("mv", sb_)])

        def prefetch(lx):
            for i in range(3):
                for k in range(2):
                    P.op("sp", lambda e, i=i, k=k: e.dma_start(
                        out=vec[:, i * 2 + k:i * 2 + k + 1],
                        in_=conv_w.ap()[lx, i, k * 128:(k + 1) * 128].rearrange("(p o) -> p o", o=1)),
                        writes=[("vecs", NEXTV())], dma="vec")
            for k in range(2):
                P.op("sp", lambda e, k=k: e.dma_start(
                    out=vec[:, 6 + k:7 + k], in_=pool_scale.ap()[lx, k * 128:(k + 1) * 128].rearrange("(p o) -> p o", o=1)),
                    writes=[("vecs", NEXTV())], dma="vec")
            for k in range(8):
                P.op("sp", lambda e, k=k: e.dma_start(
                    out=vec[:, 20 + k:21 + k], in_=mix_g.ap()[lx, k * 128:(k + 1) * 128].rearrange("(p o) -> p o", o=1)),
                    writes=[("vecs", NEXTV())], dma="vec")
            wv_ = w_in.ap()[lx].rearrange("(k p) c -> p k c", p=128)
            P.op("pool", lambda e: e.dma_start(out=wsl[:, 0, :, :], in_=wv_[:, :, 0:512]),
                 writes=[("wsl", 0), ("r32", 0), ("r32", 1)], dma=("wsl", 0))
            P.op("pool", lambda e: e.dma_start(out=wsl[:, 1, :, :], in_=wv_[:, :, 1536:2048]),
                 writes=[("wsl", 1)], dma=("wsl", 1))
            P.op("pool", lambda e: e.dma_start(out=wsl2[:], in_=wv_[:, :, 2048:2560]),
                 writes=[("wsl", 2), ("wd", 0)], dma=("wsl", 2))
            P.op("dve", lambda e: e.memset(PW[:], 0.0), writes=["PW"])
            for k in range(2):
                for h2 in range(2):
                    P.op("pool", lambda e, k=k, h2=h2: e.dma_start(
                        out=PW[h2 * 64:(h2 + 1) * 64, k, h2 * 64:(h2 + 1) * 64], in_=pool_w.ap()[lx, 2 * k + h2]),
                        writes=["PW"], dma="PW")
            P.op("dve", lambda e: e.tensor_scalar(out=vec[:, 8:16], in0=vec[:, 20:28], scalar1=1.0, scalar2=None, op0=ALU.mult),
                 reads=[*VECS], writes=["vgain"])

        def do_layer(l):
            if l == 0:
                prefetch(0)
            if l == 0:
                P.op("pool", lambda e: e.dma_start(out=xTo[:], in_=xT_own.ap().rearrange("(k p) t -> p k t", p=128)),
                     writes=["xTo"], dma="xTo")
                P.op("pool", lambda e: e.dma_start(out=xh[:], in_=xT_halo.ap().rearrange("(k p) t -> p k t", p=128)),
                     writes=["xh"], dma="xh")
            else:
                gvt = galls[4].ap().rearrange("(r p) w -> p r w", p=128)
                P.op("dve", lambda e: e.memset(xh[:], 0.0), writes=["xh"])

                def hv(ap3, par):
                    return ap3.rearrange("p k (g two t) -> p (k g) two t", two=2, t=16)[:, :, par, :]

                def hvk(ap3, k, par, g0, g1):
                    return ap3[:, k, :].rearrange("p (g two t) -> p g two t", two=2, t=16)[:, g0:g1, par, :]

                for rho in range(4):
                    tb = rho % 2
                    ccr = rho
                    P.op("sp", lambda e, rho=rho, tb=tb: e.dma_start(
                        out=tls[:, tb, :, :], in_=gvt[:, rho, :].rearrange("p (k t) -> p k t", k=8)),
                        reads=[("gall", 4)], writes=[("tls", tb)], dma=("tls", tb))
                    tl = tls[:, tb, :, :]
                    cands = []
                    if ccr <= 2:
                        cands.append((ccr + 1, 0, 0))
                    if ccr >= 1:
                        cands.append((ccr - 1, 1, 1))
                    if ccr == 3:
                        cands.append((3, 1, 0))
                    for (wc, dp, sp_) in cands:
                        P.op("dve", lambda e, wc=wc, dp=dp, sp_=sp_, tl=tl: e.scalar_tensor_tensor(
                            out=hv(xh[:], dp), in0=hv(tl, sp_), scalar=vcol(40 + wc), in1=hv(xh[:], dp),
                            op0=ALU.mult, op1=ALU.add),
                            reads=[("tls", tb), "bcw", "xh"], writes=["xh"])
                    if ccr == 0 and NG > 1:
                        for k in range(8):
                            P.op("dve", lambda e, k=k, tl=tl: e.scalar_tensor_tensor(
                                out=hvk(xh[:], k, 0, 1, NG), in0=hvk(tl, k, 1, 0, NG - 1), scalar=vcol(40),
                                in1=hvk(xh[:], k, 0, 1, NG), op0=ALU.mult, op1=ALU.add),
                                reads=[("tls", tb), "bcw", "xh"], writes=["xh"])
            P.op("sp", lambda e: e.dma_start(out=inv[:], in_=invcnt.ap().rearrange("p (k b t) -> p k b t", k=2, t=128)),
                 writes=["inv"], dma="inv")
            def proj(si, cc, rhs_fn, n, evac, xkeys=("xTo",)):
                b = gp_bank()
                pb = bank(b)
                for k in range(8):
                    P.op("pe", lambda e, k=k: e.matmul(pb[:, 0:n], (wsl2[:, k, cc:cc + 128] if si == 2 else wsl[:, si, k, cc:cc + 128]), rhs_fn(k),
                                                      start=(k == 0), stop=(k == 7)),
                         reads=[("wsl", si), *xkeys], writes=[("bank", b)])
                evac(pb, ("bank", b))

            for j in range(4):
                for tt in range(NT5):
                    def ev(pb, key, j=j, tt=tt):
                        P.op("act", lambda e: e.activation(out=QT[:, j, tt * TW:(tt + 1) * TW], in_=pb[:, 0:TW],
                                                           func=AF.Identity, scale=0.125),
                             reads=[key], writes=[("QT", j, tt)])
                    proj(0, j * 128, lambda k, tt=tt: xTo[:, k, tt * TW:(tt + 1) * TW], TW, ev)
            BPT = TW // 128

            def proj_cp(si, cc, dst, dkey, own=True, halo=True):
                if own:
                    for tt in range(NT5):
                        def ev(pb, key, tt=tt):
                            P.op("act", lambda e: e.activation(
                                out=dst[:, tt * BPT:(tt + 1) * BPT, 16:144],
                                in_=pb[:, 0:TW].rearrange("p (b t) -> p b t", t=128), func=AF.Copy),
                                reads=[key], writes=[dkey])
                        proj(si, cc, lambda k, tt=tt: xTo[:, k, tt * TW:(tt + 1) * TW], TW, ev)
                if halo:
                    def ev2(pb, key):
                        P.op("dve", lambda e: e.tensor_copy(
                            out=dst[:, :, 0:16], in_=pb[:, 0:NH].rearrange("p (b t) -> p b t", t=16)),
                            reads=[key], writes=[dkey])
                    proj(si, cc, lambda k: xh[:, k, :], NH, ev2, xkeys=("xh",))

            for k in range(2):
                proj_cp(1, k * 128, cB, "cB", halo=False)
                proj_cp(1, 256 + k * 128, cC, "cC", halo=False)
                proj_cp(2, k * 128, cH, "cH", halo=False)
                proj_cp(1, 256 + k * 128, cC, "cC", own=False)
                proj_cp(2, k * 128, cH, "cH", own=False)
                P.op("dve", lambda e: e.tensor_tensor(out=cC[:], in0=cC[:], in1=cH[:], op=ALU.mult),
                     reads=["cC", "cH"], writes=["cC"])
                cYv = cY[:].rearrange("p (b t) -> p b t", t=128)
                P.op("dve", lambda e, k=k: e.tensor_scalar(out=cYv, in0=cC[:, :, 14:142], scalar1=vcol(0 + k), scalar2=None,
                                                           op0=ALU.mult),
                     reads=["cC", *VECS], writes=["cY"])
                P.op("dve", lambda e, k=k: e.scalar_tensor_tensor(out=cYv, in0=cC[:, :, 15:143], scalar=vcol(2 + k), in1=cYv,
                                                                  op0=ALU.mult, op1=ALU.add),
                     reads=["cC", "cY", *VECS], writes=["cY"])
                P.op("dve", lambda e, k=k: e.scalar_tensor_tensor(out=cYv, in0=cC[:, :, 16:144], scalar=vcol(4 + k), in1=cYv,
                                                                  op0=ALU.mult, op1=ALU.add),
                     reads=["cC", "cY", *VECS], writes=["cY"])
                P.op("dve", lambda e: e.tensor_tensor(out=cYv, in0=cYv, in1=cB[:, :, 16:144], op=ALU.mult),
                     reads=["cY", "cB"], writes=["cY"])
                for tt in range(NT5):
                    sl = slice(tt * TW, (tt + 1) * TW)
                    rmsnorm("cY", cY[:, sl], TW, [(0, TW, mixn[:, 4 + k, sl], 4 + k, ("mixn", 4 + k, tt))], l)
            for k in range(2):
                proj_cp(2, 256 + k * 128, cC, "cC")
                nlo = 2 * k + 1
                bufs = [cH, cB]
                bkeys = ["cH", "cB"]
                src, skey = cC, "cC"
                for j in range(nlo + 1):
                    dst_, dkey = bufs[j % 2], bkeys[j % 2]
                    sh = 1 << j
                    lo = (1 << (j + 1)) - 1
                    p0 = 0 if j < nlo else 64
                    P.op("dve", lambda e, dst_=dst_, src=src, sh=sh, lo=lo, p0=p0: e.tensor_tensor(
                        out=dst_[p0:128, :, lo:144], in0=src[p0:128, :, lo:144], in1=src[p0:128, :, lo - sh:144 - sh], op=ALU.add),
                        reads=[skey], writes=[dkey])
                    src, skey = dst_, dkey
                for (p0, p1, sb_, sk) in ((0, 64, cH, "cH"), (64, 128, cB, "cB")):
                    P.op("dve", lambda e, p0=p0, p1=p1, sb_=sb_, k=k: e.tensor_tensor(
                        out=sb_[p0:p1, :, 16:144], in0=sb_[p0:p1, :, 16:144], in1=inv[p0:p1, k, :, :], op=ALU.mult),
                        reads=[sk, "inv"], writes=[sk])
                    P.op("dve", lambda e, p0=p0, p1=p1, sb_=sb_: e.tensor_tensor(
                        out=pbf[p0:p1, :].rearrange("p (b t) -> p b t", t=128), in0=sb_[p0:p1, :, 16:144],
                        in1=cC[p0:p1, :, 16:144], op=ALU.subtract),
                        reads=[sk, "cC"], writes=["pbf"])
                for tt in range(NT5):
                    sl = slice(tt * TW, (tt + 1) * TW)
                    b = gp_bank()
                    pb = bank(b)
                    P.op("pe", lambda e, k=k, sl=sl, pb=pb: e.matmul(pb[:, 0:TW], PW[:, k, :], pbf[:, sl], start=True, stop=True),
                         reads=["pbf", "PW"], writes=[("bank", b)])
                    P.op("act", lambda e, k=k, sl=sl, pb=pb: e.activation(out=cY[:, sl], in_=pb[:, 0:TW], func=AF.Identity,
                                                                          scale=vcol(6 + k)),
                         reads=[("bank", b), *VECS], writes=["cY"])
                    rmsnorm("cY", cY[:, sl], TW, [(0, TW, mixn[:, 6 + k, sl], 6 + k, ("mixn", 6 + k, tt))], l)

            P.barrier()
            P.op("pool", lambda e: e.dma_start(out=msk[:], in_=masks.ap().rearrange("p (j t) -> p j t", t=256)),
                 writes=["msk"], dma="mskA")
            tile_ctr = [0]
            def c1_wkv(hx):
                P.op("pool", lambda e: e.dma_start(
                    out=wkv[:, 0, :, :], in_=w_in.ap()[l].rearrange("(k p) c -> p k c", p=128)[:, :, 512 + hx * 256:768 + hx * 256]),
                    writes=["wk"], dma="wk")
                P.op("pool", lambda e: e.dma_start(
                    out=wkv[:, 1, :, :], in_=w_in.ap()[l].rearrange("(k p) c -> p k c", p=128)[:, :, 1024 + hx * 256:1280 + hx * 256]),
                    writes=["wv"], dma="wv")

            def c1_xs(st, sb):
                if l == 0:
                    P.op("pool", lambda e: e.dma_start(
                        out=xs[:, sb, :, 0:XW], in_=xT_full.ap().rearrange("(k p) t -> p k t", p=128)[:, :, st * XW:(st + 1) * XW]),
                        writes=[("xs", sb)], dma=("xs", sb))
                else:
                    ccp, m = st // (T // XW), st % (T // XW)
                    for i4 in range(4):
                        gvi = galls[i4].ap().rearrange("(r p) w -> p r w", p=128)
                        P.op("sp", lambda e, i4=i4, gvi=gvi: e.dma_start(
                            out=xs[:, sb, 2 * i4:2 * i4 + 2, 0:XW],
                            in_=gvi[:, ccp, :].rearrange("p (k t) -> p k t", k=2)[:, :, m * XW:(m + 1) * XW]),
                            reads=[("gall", i4)], writes=[("xs", sb)], dma=("xsh", sb))

            def do_half(hh):
                pre = (hh == 1)
                if not pre:
                    c1_wkv(hh)
                TPR = T // XW
                order = [ccp * TPR + m for m in range(TPR) for ccp in range(4)]
                pending = []

                def k_piece(st, sb, jj):
                    b = gp_bank()
                    pb = bank(b)
                    for k in range(8):
                        P.op("pe", lambda e, k=k: e.matmul(
                            pb[:, 0:XW], wkv[:, 0, k, jj * 128:(jj + 1) * 128], xs[:, sb, k, 0:XW], start=(k == 0), stop=(k == 7)),
                            reads=["wk", ("xs", sb)], writes=[("bank", b)])
                    P.op("act", lambda e: e.activation(
                        out=KT[:, jj, st * XW:(st + 1) * XW], in_=pb[:, 0:XW], func=AF.Copy),
                        reads=[("bank", b)], writes=[("KT", st)])

                def v_piece(st, sb, b2):
                    b = gp_bank()
                    pb = bank(b)
                    for bl in range(2):
                        blk = b2 * 2 + bl
                        for k in range(8):
                            P.op("pe", lambda e, k=k, blk=blk, bl=bl: e.matmul(
                                pb[:, bl * 256:(bl + 1) * 256], xs[:, sb, k, blk * 128:(blk + 1) * 128], wkv[:, 1, k, :],
                                start=(k == 0 and bl == 0), stop=(k == 7), skip_group_check=True),
                                reads=["wv", ("xs", sb)], writes=[("bank", b)])
                    P.op("dve", lambda e: e.tensor_copy(
                        out=Vt[:, st * BPX + b2 * 2:st * BPX + b2 * 2 + 2, :], in_=pb[:, 0:512].rearrange("p (b c) -> p b c", c=256)),
                        reads=[("bank", b)], writes=[("V", st)])

                for idx, st in enumerate(order):
                    sb = idx % 2
                    lvl = idx // 4
                    if not (pre and idx < 2):
                        pending.append((lvl, lambda st=st, sb=sb: c1_xs(st, sb)))
                    for jj in range(2):
                        pending.append((lvl, lambda st=st, sb=sb, jj=jj: k_piece(st, sb, jj)))
                    for b2 in range(BPX // 2):
                        pending.append((lvl, lambda st=st, sb=sb, b2=b2: v_piece(st, sb, b2)))
                state = {"pref": False}

                def c1_pump(level_needed, trickle):
                    while pending and pending[0][0] <= level_needed:
                        pending.pop(0)[1]()
                    for _ in range(trickle):
                        if pending:
                            pending.pop(0)[1]()
                    if not pending and hh == 0 and not state["pref"]:
                        state["pref"] = True
                        c1_wkv(1)
                        c1_xs(order[0], 0)
                        c1_xs(order[1], 1)

                tiles = []
                for g in range(NG):
                    for kb in range(8 * g + 7, -1, -1):
                        tiles.append((g, kb))
                NTL = len(tiles)

                def hiv(ap2):
                    return ap2.rearrange("p (h c) -> p h c", c=256)[:, :, 128:256]

                def lov(ap2):
                    return ap2.rearrange("p (h c) -> p h c", c=256)[:, :, 0:128]

                def stage1a(i):
                    g, kb = tiles[i]
                    Z = zw(i)
                    zk = ("zw", i % 2)
                    j = kb - 8 * g
                    ho = SKIPLO and j >= 4
                    o0 = 128 if ho else 0
                    q = slice(g * 256 + o0, g * 256 + 256)
                    sg = sigma(kb)
                    ks = slice(sg * 128, sg * 128 + 128)
                    ktk = ("KT", sg // BPX)
                    qk = [("QT", 2 * hh, (g * 256) // TW), ("QT", 2 * hh + 1, (g * 256) // TW)]
                    spec = [(0, 0, 0, True), (512, 64, 0, True), (256, 0, 1, False), (768, 64, 1, False)]
                    for (c0, p0, ch, st_) in spec:
                        P.op("pe", lambda e, c0=c0, p0=p0, ch=ch, st_=st_: e.matmul(
                            Z[:, c0 + o0:c0 + 256], KT[p0:p0 + 64, ch, ks], QT[p0:p0 + 64, 2 * hh + ch, q],
                            start=st_, stop=False, skip_group_check=True),
                            reads=[ktk] + qk, writes=[zk])
                    if j >= 0:
                        for hb in range(4):
                            P.op("pe", lambda e, hb=hb, j=j: e.matmul(
                                Z[:, hb * 256 + o0:(hb + 1) * 256], IDENT, msk[:, j, o0:256], start=False, stop=False,
                                skip_group_check=True),
                                reads=["cst", "msk"], writes=[zk])
                    if ho:
                        P.op("act", lambda e: e.activation(out=hiv(Eb[:, i % 2, :]), in_=hiv(Z), func=AF.Exp),
                             reads=[zk], writes=[("E", i % 2)])
                    else:
                        P.op("act", lambda e: e.activation(out=Eb[:, i % 2, :], in_=Z, func=AF.Exp),
                             reads=[zk], writes=[("E", i % 2)])

                def stage1b(i):
                    g, kb = tiles[i]
                    first = (kb == 8 * g + 7)
                    j = kb - 8 * g
                    ho = SKIPLO and j >= 4
                    Li = Lb[:, i % 2, :]
                    Cn = Cs[:, i % 3, :]
                    Co = Cs[:, (i - 1) % 3, :]
                    if ho:
                        P.op("act", lambda e: e.activation(out=hiv(Li), in_=hiv(Eb[:, i % 2, :]), func=AF.Ln, bias=1.0),
                             reads=[("E", i % 2)], writes=[("L", i % 2)])
                    else:
                        P.op("act", lambda e: e.activation(out=Li, in_=Eb[:, i % 2, :], func=AF.Ln, bias=1.0),
                             reads=[("E", i % 2)], writes=[("L", i % 2)])
                    if first:
                        if ho:
                            P.op(CSENG, lambda e: e.tensor_copy(out=hiv(Cn), in_=hiv(Li)),
                                 reads=[("L", i % 2)], writes=[("Cs", i % 3)])
                        else:
                            P.op(CSENG, lambda e: e.tensor_copy(out=Cn, in_=Li),
                                 reads=[("L", i % 2)], writes=[("Cs", i % 3)])
                    elif ho:
                        P.op(CSENG, lambda e: e.tensor_tensor(out=hiv(Cn), in0=hiv(Co), in1=hiv(Li), op=ALU.add),
                             reads=[("L", i % 2), ("Cs", (i - 1) % 3)], writes=[("Cs", i % 3)])
                    elif SKIPLO and j == 3:
                        P.op(CSENG, lambda e: e.tensor_tensor(out=hiv(Cn), in0=hiv(Co), in1=hiv(Li), op=ALU.add),
                             reads=[("L", i % 2), ("Cs", (i - 1) % 3)], writes=[("Cs", i % 3)])
                        P.op(CSENG, lambda e: e.tensor_copy(out=lov(Cn), in_=lov(Li)),
                             reads=[("L", i % 2)], writes=[("Cs", i % 3)])
                    else:
                        P.op(CSENG, lambda e: e.tensor_tensor(out=Cn, in0=Co, in1=Li, op=ALU.add),
                             reads=[("L", i % 2), ("Cs", (i - 1) % 3)], writes=[("Cs", i % 3)])

                def stage2(i):
                    g, kb = tiles[i]
                    first = (kb == 8 * g + 7)
                    j = kb - 8 * g
                    ho = SKIPLO and j >= 4
                    Z = zw(i)
                    zk = ("zw", i % 2)
                    if ho:
                        for r4 in range(4):
                            cs_ = slice(r4 * 256 + 128, r4 * 256 + 256)
                            P.op("pe", lambda e, cs_=cs_: e.matmul(Z[:, cs_], NU, Lb[:, i % 2, cs_], start=False, stop=False,
                                                                  skip_group_check=True),
                                 reads=["cst", ("L", i % 2)], writes=[zk])
                    else:
                        for hb in range(2):
                            P.op("pe", lambda e, hb=hb: e.matmul(
                                Z[:, hb * 512:(hb + 1) * 512], NU, Lb[:, i % 2, hb * 512:(hb + 1) * 512],
                                start=False, stop=False, skip_group_check=True),
                                reads=["cst", ("L", i % 2)], writes=[zk])
                    if not first:
                        if ho or (SKIPLO and j == 3):
                            for r4 in range(4):
                                cs_ = slice(r4 * 256 + 128, r4 * 256 + 256)
                                P.op("pe", lambda e, cs_=cs_: e.matmul(Z[:, cs_], NONES, Cs[:, (i - 1) % 3, cs_], start=False, stop=False,
                                                                      skip_group_check=True),
                                     reads=["cst", ("Cs", (i - 1) % 3)], writes=[zk])
                        else:
                            for hb in range(2):
                                P.op("pe", lambda e, hb=hb: e.matmul(
                                    Z[:, hb * 512:(hb + 1) * 512], NONES, Cs[:, (i - 1) % 3, hb * 512:(hb + 1) * 512],
                                    start=False, stop=False, skip_group_check=True),
                                    reads=["cst", ("Cs", (i - 1) % 3)], writes=[zk])
                    if ho:
                        P.op("act", lambda e: e.activation(out=hiv(Ab[:, i % 2, :]), in_=hiv(Z), func=AF.Exp),
                             reads=[zk], writes=[("A", i % 2)])
                    else:
                        P.op("act", lambda e: e.activation(out=Ab[:, i % 2, :], in_=Z, func=AF.Exp),
                             reads=[zk], writes=[("A", i % 2)])

                def stage3(i):
                    g, kb = tiles[i]
                    first = (kb == 8 * g + 7)
                    last = (kb == 0)
                    j = kb - 8 * g
                    ho = SKIPLO and j >= 4
                    o0 = 128 if ho else 0
                    pob = 4 + (g % 2)
                    PO = bank(pob)
                    pk = ("bank", pob)
                    spec = [(0, 0, 0, 0, True), (64, 0, 64, 512, True), (0, 256, 128, 256, False), (64, 256, 192, 768, False)]
                    for (p0, c0, v0, a0, st_) in spec:
                        P.op("pe", lambda e, p0=p0, c0=c0, v0=v0, a0=a0, st_=st_: e.matmul(
                            PO[p0:p0 + 64, c0 + o0:c0 + 256], Vt[:, sigma(kb), v0:v0 + 64], Ab[:, i % 2, a0 + o0:a0 + 256],
                            start=(first and st_), stop=False, skip_group_check=True),
                            reads=[("V", sigma(kb) // BPX), ("A", i % 2)], writes=[pk])
                    if last:
                        P.op("dve", lambda e: e.tensor_copy(out=yat[:], in_=PO), reads=[pk], writes=["yat"])
                        tt = (g * 256) // TW
                        rmsnorm("yat", yat[:], 512,
                                [(0, 256, mixn[:, 2 * hh, g * 256:g * 256 + 256], 2 * hh, ("mixn", 2 * hh, tt)),
                                 (256, 512, mixn[:, 2 * hh + 1, g * 256:g * 256 + 256], 2 * hh + 1, ("mixn", 2 * hh + 1, tt))], l)

                for i in range(NTL + 2):
                    if i < NTL:
                        c1_pump(((2 * tiles[i][0] + 1) * 128) // XW, 0)
                        stage1a(i)
                    if 0 <= i - 1 < NTL:
                        stage2(i - 1)
                    if i < NTL:
                        stage1b(i)
                    if 0 <= i - 2 < NTL:
                        stage3(i - 2)
                    if i < NTL:
                        c1_pump(-1, 1 if (pending and pending[0][0] <= 1) or i % 2 == 0 else 0)
                c1_pump(10 ** 6, 0)
                P.barrier()

            for hh_ in range(2):
                do_half(hh_)

            if DEBUG:
                P.op("sp", lambda e: e.dma_start(out=dbg_qt.ap().rearrange("p (k t) -> p k t", k=4), in_=QT[:]),
                     reads=[], dma="dbg3")
                P.barrier()
            P.op("pool", lambda e: e.dma_start(out=wo[:], in_=w_o.ap()[l].rearrange("(k p) c -> p k c", p=128)),
                 writes=["wo"], dma="wo")
            P.op("sp", lambda e: e.dma_start(out=gb[:, 0, :], in_=dram_ap(ln1_g, l * D, [[0, 128], [1, D]])), writes=["gb"], dma="gb0")
            P.op("sp", lambda e: e.dma_start(out=gb[:, 1, :], in_=dram_ap(ln1_b, l * D, [[0, 128], [1, D]])), writes=["gb1"], dma="gb1")

            def transposes(b):
                pt = zw(b)
                for k in range(8):
                    P.op("pe", lambda e, k=k: e.transpose(pt[:, k * 128:(k + 1) * 128], x1[:, b, k * 128:(k + 1) * 128], idf[:]),
                         reads=[("x1", b), "idf"], writes=[("zw", b % 2)])
                P.op("act", lambda e: e.activation(out=x1T[:, :, b * 128:(b + 1) * 128],
                                                   in_=pt.rearrange("p (k t) -> p k t", t=128), func=AF.Copy),
                     reads=[("zw", b % 2)], writes=[("x1T", b)])

            def d_s1(b):
                P.op("sp", lambda e, b=b: e.dma_start(out=xin[:, b % 2, :], in_=(x_own if l == 0 else xsp).ap()[b * 128:(b + 1) * 128, :]),
                     writes=[("xin", b % 2)], dma=("xin", b % 2))
                mo = ps[:, 2048 + (b % 2) * 1024:2048 + (b % 2) * 1024 + 1024]
                mk = ("mo", b % 2)
                for hf in range(2):
                    for k in range(8):
                        P.op("pe", lambda e, k=k, hf=hf, b=b, mo=mo: e.matmul(
                            mo[:, hf * 512:(hf + 1) * 512], mixn[:, k, b * 128:(b + 1) * 128], wo[:, k, hf * 512:(hf + 1) * 512],
                            start=(k == 0), stop=(k == 7)),
                            reads=["wo", ("mixn", k, (b * 128) // TW)], writes=[mk])

            def d_s2(b):
                mo = ps[:, 2048 + (b % 2) * 1024:2048 + (b % 2) * 1024 + 1024]
                mk = ("mo", b % 2)
                for hf in range(2):
                    P.op("dve", lambda e, b=b, hf=hf, mo=mo: e.scalar_tensor_tensor(
                        out=x1[:, b, hf * 512:(hf + 1) * 512], in0=xin[:, b % 2, hf * 512:(hf + 1) * 512], scalar=ALPHA,
                        in1=mo[:, hf * 512:(hf + 1) * 512], op0=ALU.mult, op1=ALU.add),
                        reads=[("xin", b % 2), mk], writes=[("x1", b)])
                ln_stats(b)

            def d_s3(b):
                ln_apply(b, "gb1")
                transposes(b)

            for i in range(NB + 2):
                if i < NB:
                    d_s1(i)
                if 0 <= i - 1 < NB:
                    d_s2(i - 1)
                if 0 <= i - 2 < NB:
                    d_s3(i - 2)
            if DEBUG:
                P.op("sp", lambda e: e.dma_start(out=dbg_mixn.ap().rearrange("p (k t) -> p k t", k=8), in_=mixn[:]),
                     reads=[], dma="dbg1")
                for b in range(NB):
                    P.op("sp", lambda e, b=b: e.dma_start(out=dbg_x1.ap()[b * 128:(b + 1) * 128, :], in_=x1[:, b, :]),
                         reads=[("x1", b)], dma="dbg2")
            P.barrier()

            for qq in range(4):
                wb = qq % 2
                P.op("pool", lambda e, qq=qq, wb=wb: e.dma_start(
                    out=wu[:, wb, :, :], in_=w_up.ap()[l].rearrange("(k p) f -> p k f", p=128)[:, :, qq * 1024:(qq + 1) * 1024]),
                    writes=[("wu", wb)], dma=("wu", wb))
                P.op("pool", lambda e, qq=qq, wb=wb: e.dma_start(
                    out=wd[:, wb, :, :], in_=w_down.ap()[l, qq * 1024:(qq + 1) * 1024, :].rearrange("(f p) d -> p f d", p=128)),
                    writes=[("wd", wb)], dma=("wd", wb))
                for tt in range(NT5):
                    hb_ = tt % 2
                    for fc in range(8):
                        b = gp_bank()
                        pb = bank(b)
                        for k in range(8):
                            P.op("pe", lambda e, k=k, fc=fc, wb=wb, tt=tt, pb=pb: e.matmul(
                                pb[:, 0:TW], wu[:, wb, k, fc * 128:(fc + 1) * 128], x1T[:, k, tt * TW:(tt + 1) * TW],
                                start=(k == 0), stop=(k == 7)),
                                reads=[("wu", wb)] + [("x1T", bb) for bb in range(tt * BPT, (tt + 1) * BPT)], writes=[("bank", b)])
                        rb = fc % 2
                        P.op("act", lambda e, rb=rb, pb=pb: e.activation(out=r32[:, rb, 0:TW], in_=pb[:, 0:TW], func=AF.Relu),
                             reads=[("bank", b)], writes=[("r32", rb)])
                        P.op("dve", lambda e, rb=rb, fc=fc, hb_=hb_: e.tensor_tensor(
                            out=hid[:, hb_, fc, :], in0=r32[:, rb, 0:TW], in1=r32[:, rb, 0:TW], op=ALU.mult),
                            reads=[("r32", rb)], writes=[("hid", hb_)])
                    for bl in range(BPT):
                        blk = tt * BPT + bl
                        for hf in range(2):
                            b = 4 + ((bl * 2 + hf) % 2)
                            pb = bank(b)
                            for fc in range(8):
                                P.op("pe", lambda e, fc=fc, bl=bl, hf=hf, hb_=hb_, wb=wb, pb=pb: e.matmul(
                                    pb[:, 0:512], hid[:, hb_, fc, bl * 128:(bl + 1) * 128], wd[:, wb, fc, hf * 512:(hf + 1) * 512],
                                    start=(fc == 0), stop=(fc == 7)),
                                    reads=[("hid", hb_), ("wd", wb)], writes=[("bank", b)])
                            xsl = x1[:, blk, hf * 512:(hf + 1) * 512]
                            if qq == 0:
                                P.op("dve", lambda e, xsl=xsl, pb=pb: e.scalar_tensor_tensor(
                                    out=xsl, in0=xsl, scalar=ALPHA, in1=pb[:, 0:512], op0=ALU.mult, op1=ALU.add),
                                    reads=[("bank", b), ("x1", blk)], writes=[("x1", blk)])
                            else:
                                P.op("dve", lambda e, xsl=xsl, pb=pb: e.tensor_tensor(out=xsl, in0=xsl, in1=pb[:, 0:512], op=ALU.add),
                                     reads=[("bank", b), ("x1", blk)], writes=[("x1", blk)])
            P.op("sp", lambda e: e.dma_start(out=gb[:, 0, :], in_=dram_ap(ln2_g, l * D, [[0, 128], [1, D]])), writes=["gb"], dma="gb0")
            P.op("sp", lambda e: e.dma_start(out=gb[:, 1, :], in_=dram_ap(ln2_b, l * D, [[0, 128], [1, D]])), writes=["gb2"], dma="gb1")
            if l < NL - 1:
                prefetch(l + 1)

            def fin(b):
                ln_apply(b, "gb2")
                if l == NL - 1:
                    P.op("sp", lambda e, b=b: e.dma_start(out=out.ap()[b * 128:(b + 1) * 128, :], in_=x1[:, b, :]),
                         reads=[("x1", b)], dma=("out", b % 4))
                else:
                    P.op("sp", lambda e, b=b: e.dma_start(out=xsp.ap()[b * 128:(b + 1) * 128, :], in_=x1[:, b, :]),
                         reads=[("x1", b)], dma=("xsp", b % 4))
                    transposes(b)

            for b in range(NB):
                ln_stats(b)
                if b >= 1:
                    fin(b - 1)
            fin(NB - 1)
            if l < NL - 1:
                allx = [("x1T", b) for b in range(NB)]
                for k in range(8):
                    P.op("sp", lambda e, k=k: e.dma_start(
                        out=locs[4].ap()[:, k * NH:(k + 1) * NH].rearrange("p (s t) -> p s t", t=16),
                        in_=xTo[:, k, :].rearrange("p (s t) -> p s t", t=128)[:, :, 112:128]),
                        reads=allx, writes=[("loc", 4)], dma=("loc", 4))
                for i4 in range(4):
                    P.op("sp", lambda e, i4=i4: e.dma_start(out=locs[i4].ap().rearrange("p (k t) -> p k t", k=2),
                                                            in_=xTo[:, 2 * i4:2 * i4 + 2, :]),
                         reads=allx, writes=[("loc", i4)], dma=("loc", i4))
                for i5 in (4, 0, 1, 2, 3):
                    P.op("pool", lambda e, i5=i5: e.collective_compute(
                        "AllGather", ALU.bypass, replica_groups=[[0, 1, 2, 3], [4, 5, 6, 7]],
                        ins=[locs[i5].ap()], outs=[galls[i5].ap()]),
                        reads=[("loc", i5)], writes=[("gall", i5)], dma=("cc", i5), dma_amt=1)
                P.barrier(skip_dma_keys=[("cc", i5) for i5 in range(5)])
            else:
                P.barrier()

        for l_ in range(NL):
            do_layer(l_)
        P.final_wait("sp")

        block = es.enter_context(nc.Block())
        block.tensor(lambda e: P.emit("pe", e))
        block.scalar(lambda e: P.emit("act", e))
        block.vector(lambda e: P.emit("dve", e))
        block.gpsimd(lambda e: P.emit("pool", e))
        block.sync(lambda e: P.emit("sp", e))
    return nc


def _consts():
    c = np.zeros((128, 4, 128), np.float32)
    c[:, 0, :] = np.eye(128, dtype=np.float32)
    jj = np.arange(128)[:, None]
    ss = np.arange(128)[None, :]
    c[:, 1, :] = -(jj >= ss).astype(np.float32)
    c[:, 2, :] = -1.0
    bo = np.zeros((128, 128), np.float32)
    bo[:64, :64] = 1.0
    bo[64:, 64:] = 1.0
    c[:, 3, :] = bo
    return c.reshape(128, 512)


def _core_layout(NG, cc):
    pos = []
    for g in range(NG):
        pos += [8 * g + cc, 8 * g + 7 - cc]
    return pos


def _masks(cc):
    m = np.zeros((128, 8, 256), np.float32)
    s = np.arange(128)[:, None]
    t = np.arange(128)[None, :]
    tri = np.where(s < t, 0.0, NEG).astype(np.float32)
    for j in range(8):
        for half, qpos in ((0, cc), (1, 7 - cc)):
            sl = slice(half * 128, (half + 1) * 128)
            if j < qpos:
                m[:, j, sl] = 0.0
            elif j == qpos:
                m[:, j, sl] = tri
            else:
                m[:, j, sl] = NEG
    return m.reshape(128, 8 * 256)


def _bcw(b, cc):
    w = np.zeros((128, 16), np.float32)
    w[:, b] = 1.0
    w[:, 8 + cc] = 1.0
    return w


def _invcnt(NG, cc):
    pos = _core_layout(NG, cc)
    NB = len(pos)
    inv = np.zeros((128, 2, NB, 128), np.float32)
    for k in range(2):
        for h2 in range(2):
            w = 1 << (2 * k + h2 + 1)
            for bi, p in enumerate(pos):
                tpos = p * 128 + np.arange(128)
                cnt = np.minimum(tpos + 1, w).astype(np.float32)
                inv[h2 * 64:(h2 + 1) * 64, k, bi, :] = (1.0 / cnt)[None, :]
    return inv.reshape(128, 2 * NB * 128)


def _run_layers(x, wts, NG, layers, nc_cache={}):
    B, S, _ = x.shape
    NL = len(layers)
    key = (NG, NL)
    if key not in nc_cache:
        nc_cache[key] = build_nc(NG, NL)
    nc = nc_cache[key]
    consts = _consts()
    in_maps = []
    lay = []
    for c in range(8):
        b, cc = c // 4, c % 4
        pos = _core_layout(NG, cc)
        lay.append((b, pos))
        xb = x[b]
        x_own = np.concatenate([xb[p * 128:(p + 1) * 128] for p in pos], axis=0)
        halo = []
        for p in pos:
            if p == 0:
                halo.append(np.zeros((16, D), np.float32))
            else:
                halo.append(xb[p * 128 - 16:p * 128])
        halo = np.concatenate(halo, axis=0)
        m = {
            "x_own": np.ascontiguousarray(x_own),
            "xT_own": np.ascontiguousarray(x_own.T),
            "xT_halo": np.ascontiguousarray(halo.T),
            "xT_full": np.ascontiguousarray(np.concatenate(
                [xb[p * 128:(p + 1) * 128] for c2 in range(4) for p in _core_layout(NG, c2)], axis=0).T),
            "bcw": _bcw(b, cc),
            "invcnt": _invcnt(NG, cc),
            "masks": _masks(cc),
            "consts": consts,
        }
        for k, v in wts.items():
            m[k] = np.ascontiguousarray(v[layers])
        in_maps.append(m)
    res = run_bass_kernel_spmd(nc, in_maps, core_ids=list(range(8)))
    if DEBUG:
        _run_layers.dbg = res.results
    outp = np.empty_like(x)
    for c in range(8):
        b, pos = lay[c]
        o = res.results[c]["out"]
        for i, p in enumerate(pos):
            outp[b, p * 128:(p + 1) * 128] = o[i * 128:(i + 1) * 128]
    return outp


def kernel(x, w_in, conv_w, pool_w, pool_scale, mix_norm_g, w_o, ln1_g, ln1_b, w_up, w_down, ln2_g, ln2_b):
    x = np.asarray(x, np.float32)
    wts = {"w_in": w_in, "conv_w": conv_w, "pool_w": pool_w, "pool_scale": pool_scale, "mix_norm_g": mix_norm_g,
           "w_o": w_o, "ln1_g": ln1_g, "ln1_b": ln1_b, "w_up": w_up, "w_down": w_down, "ln2_g": ln2_g, "ln2_b": ln2_b}
    wts = {k: np.asarray(v, np.float32) for k, v in wts.items()}
    S = x.shape[1]
    NG = S // 1024
    depth = wts["w_in"].shape[0]
    if FUSED:
        return _run_layers(x, wts, NG, list(range(depth)))
    for l in range(depth):
        x = _run_layers(x, wts, NG, [l])
    return x
```
